# Optimizing a Trainium2 kernel written in Bass

```python
import math
import jax, jax.numpy as jnp
from jax import lax
import numpy as np

D_MODEL = 1024
BATCH = 8
SEQ = 4096
DEPTH = 4

N_EVEN = (DEPTH + 1) // 2
N_ODD = DEPTH // 2
NORM_EPS = 1e-6

LRU_WIDTH = D_MODEL
LRU_HEADS = 8
LRU_HEAD_DIM = LRU_WIDTH // LRU_HEADS
LRU_CONV = 4
LRU_C = 8.0

SSD_INNER = D_MODEL
SSD_HEAD_DIM = 64
SSD_HEADS = SSD_INNER // SSD_HEAD_DIM
SSD_GROUPS = 2
SSD_HPG = SSD_HEADS // SSD_GROUPS
SSD_STATE = 128
SSD_CONV = 4
SSD_CHUNK = 128
SSD_CONV_DIM = SSD_INNER + 2 * SSD_GROUPS * SSD_STATE

EVEN_IN = 2 * LRU_WIDTH + SSD_INNER + SSD_CONV_DIM + SSD_HEADS
EVEN_MIX = LRU_WIDTH + SSD_INNER

SC_WIDTH = D_MODEL
SC_CONV = 3

PEER_HEADS = 8
PEER_NKEYS = 128
PEER_NEXPERTS = PEER_NKEYS * PEER_NKEYS
PEER_TOPK = 16
PEER_QDIM = 256
PEER_HALF = PEER_QDIM // 2
PEER_BLOCK = 128

kernel_name = "hybrid_rglru_ssd_shortconv_peer"


def rmsnorm(x, g):
    x32 = x.astype(jnp.float32)
    y = x32 * lax.rsqrt(jnp.mean(x32 * x32, axis=-1, keepdims=True) + NORM_EPS)
    return (y * g.astype(jnp.float32)).astype(x.dtype)


def causal_dwconv(x, w):
    k_width, ch = w.shape
    return lax.conv_general_dilated(
        x, w[:, None, :].astype(x.dtype), window_strides=(1,),
        padding=[(k_width - 1, 0)], dimension_numbers=('NWC', 'WIO', 'NWC'),
        feature_group_count=ch)


def rg_lru(x, gate_a_w, gate_a_b, gate_x_w, gate_x_b, lam):
    bsz, s, w = x.shape
    xh = x.reshape(bsz, s, LRU_HEADS, LRU_HEAD_DIM)
    r = jax.nn.sigmoid(jnp.einsum('bshi,hij->bshj', xh, gate_a_w).reshape(bsz, s, w) + gate_a_b)
    i = jax.nn.sigmoid(jnp.einsum('bshi,hij->bshj', xh, gate_x_w).reshape(bsz, s, w) + gate_x_b)
    log_a = -LRU_C * r.astype(jnp.float32) * jax.nn.softplus(-lam.astype(jnp.float32))
    a = jnp.exp(log_a)
    mult = jnp.sqrt(jnp.maximum(-jnp.expm1(2.0 * log_a), 0.0))
    b = mult * (i * x).astype(jnp.float32)

    def combine(left, right):
        a1, b1 = left
        a2, b2 = right
        return a1 * a2, a2 * b1 + b2

    _, h = lax.associative_scan(combine, (a, b), axis=1)
    return h.astype(x.dtype)


def ssd_scan(x, dt, a, bmat, cmat):
    bsz, s = x.shape[:2]
    nc, L = s // SSD_CHUNK, SSD_CHUNK
    x = x.reshape(bsz, nc, L, SSD_GROUPS, SSD_HPG, SSD_HEAD_DIM)
    dt = dt.reshape(bsz, nc, L, SSD_GROUPS, SSD_HPG)
    bm = bmat.reshape(bsz, nc, L, SSD_GROUPS, SSD_STATE)
    cm = cmat.reshape(bsz, nc, L, SSD_GROUPS, SSD_STATE)
    xdt = x * dt[..., None]
    da_cum = jnp.cumsum(dt * a.reshape(SSD_GROUPS, SSD_HPG), axis=2)

    seg = da_cum[:, :, :, None] - da_cum[:, :, None]
    causal = jnp.tril(jnp.ones((L, L), dtype=bool))[None, None, :, :, None, None]
    decay = jnp.exp(jnp.where(causal, seg, -jnp.inf))
    cb = jnp.einsum('bclgn,bcsgn->bclsg', cm, bm)
    y_diag = jnp.einsum('bclsgj,bcsgjp->bclgjp', cb[..., None] * decay, xdt)

    decay_to_end = jnp.exp(da_cum[:, :, -1:] - da_cum)
    states = jnp.einsum('bclgn,bclgjp->bcgjpn', bm, decay_to_end[..., None] * xdt)
    chunk_decay = jnp.exp(da_cum[:, :, -1])

    def step(h, inp):
        dec, st = inp
        return h * dec[..., None, None] + st, h

    h0 = jnp.zeros((bsz, SSD_GROUPS, SSD_HPG, SSD_HEAD_DIM, SSD_STATE), jnp.float32)
    _, prev = lax.scan(step, h0, (jnp.moveaxis(chunk_decay, 1, 0), jnp.moveaxis(states, 1, 0)))
    prev = jnp.moveaxis(prev, 0, 1)
    y_off = jnp.einsum('bclgn,bcgjpn->bclgjp', cm, prev) * jnp.exp(da_cum)[..., None]
    return (y_diag + y_off).reshape(bsz, s, SSD_HEADS, SSD_HEAD_DIM)


def even_mixer(h, w_in, lru_conv_w, lru_conv_b, lru_ga_w, lru_ga_b, lru_gx_w, lru_gx_b,
               lru_lam, ssd_conv_w, ssd_conv_b, ssd_dt_bias, ssd_a_log, ssd_d, ssd_norm_g, w_out):
    bsz, s, _ = h.shape
    proj = h @ w_in
    o1 = LRU_WIDTH
    o2 = o1 + LRU_WIDTH
    o3 = o2 + SSD_INNER
    o4 = o3 + SSD_CONV_DIM
    lru_gate, lru_x, ssd_z, ssd_xbc, ssd_dt = jnp.split(proj, [o1, o2, o3, o4], axis=-1)

    xa = causal_dwconv(lru_x, lru_conv_w) + lru_conv_b
    ya = jax.nn.gelu(lru_gate) * rg_lru(xa, lru_ga_w, lru_ga_b, lru_gx_w, lru_gx_b, lru_lam)

    xbc = jax.nn.silu(causal_dwconv(ssd_xbc, ssd_conv_w) + ssd_conv_b)
    xs, bs, cs = jnp.split(xbc, [SSD_INNER, SSD_INNER + SSD_GROUPS * SSD_STATE], axis=-1)
    xs32 = xs.astype(jnp.float32).reshape(bsz, s, SSD_HEADS, SSD_HEAD_DIM)
    dt = jax.nn.softplus(ssd_dt.astype(jnp.float32) + ssd_dt_bias.astype(jnp.float32))
    a = -jnp.exp(ssd_a_log.astype(jnp.float32))
    y = ssd_scan(xs32, dt, a,
                 bs.astype(jnp.float32).reshape(bsz, s, SSD_GROUPS, SSD_STATE),
                 cs.astype(jnp.float32).reshape(bsz, s, SSD_GROUPS, SSD_STATE))
    y = y + ssd_d.astype(jnp.float32)[:, None] * xs32
    y = y.reshape(bsz, s, SSD_INNER).astype(h.dtype)
    yb = rmsnorm(y * jax.nn.silu(ssd_z), ssd_norm_g)

    return jnp.concatenate([ya, yb], axis=-1) @ w_out


def odd_mixer(h, w_in, conv_w, w_out):
    b_gate, c_gate, v = jnp.split(h @ w_in, 3, axis=-1)
    return (b_gate * causal_dwconv(c_gate * v, conv_w)) @ w_out


def peer_ffn(h, w_query, sub_keys, expert_u, expert_v):
    bsz, s, d = h.shape
    tokens = h.reshape(-1, PEER_BLOCK, d)

    def block(xt):
        t = xt.shape[0]
        q = (xt @ w_query).reshape(t, PEER_HEADS, 2, PEER_HALF)
        sc = jnp.einsum('thpd,hpnd->thpn', q, sub_keys).astype(jnp.float32)
        sv, si = lax.top_k(sc, PEER_TOPK)
        cand = sv[:, :, 0, :, None] + sv[:, :, 1, None, :]
        cand_idx = si[:, :, 0, :, None] * PEER_NKEYS + si[:, :, 1, None, :]
        top_v, top_pos = lax.top_k(cand.reshape(t, PEER_HEADS, PEER_TOPK * PEER_TOPK), PEER_TOPK)
        idx = jnp.take_along_axis(cand_idx.reshape(t, PEER_HEADS, PEER_TOPK * PEER_TOPK), top_pos, axis=-1)
        g = jax.nn.softmax(top_v, axis=-1)
        act = jax.nn.gelu(jnp.einsum('thkd,td->thk', expert_u[idx], xt).astype(jnp.float32))
        wgt = (g * act).astype(xt.dtype)
        return jnp.einsum('thk,thkd->td', wgt, expert_v[idx])

    return lax.map(block, tokens).reshape(bsz, s, d)


def setup_inputs(seed: int = 0) -> dict:
    key = jax.random.key(seed)
    ks = iter(jax.random.split(key, 40))

    def nrm(shape, scale):
        return scale * jax.random.normal(next(ks), shape, jnp.float32)

    def gain(shape):
        return 1.0 + nrm(shape, 0.02)

    E, O = N_EVEN, N_ODD
    x = nrm((BATCH, SEQ, D_MODEL), 1.0)

    u = jax.random.uniform(next(ks), (E, LRU_WIDTH), jnp.float32, 0.9, 0.999)
    a_base = u ** (1.0 / LRU_C)
    lru_lambda = jnp.log(a_base) - jnp.log1p(-a_base)
    dt0 = jnp.exp(jax.random.uniform(next(ks), (E, SSD_HEADS), jnp.float32,
                                     math.log(1e-3), math.log(0.1)))
    ssd_dt_bias = dt0 + jnp.log(-jnp.expm1(-dt0))
    ssd_a_log = jnp.log(jax.random.uniform(next(ks), (E, SSD_HEADS), jnp.float32, 1.0, 16.0))

    return {
        "x": x,
        "even_norm_g": gain((E, D_MODEL)),
        "even_w_in": nrm((E, D_MODEL, EVEN_IN), D_MODEL ** -0.5),
        "lru_conv_w": nrm((E, LRU_CONV, LRU_WIDTH), LRU_CONV ** -0.5),
        "lru_conv_b": nrm((E, LRU_WIDTH), 0.01),
        "lru_gate_a_w": nrm((E, LRU_HEADS, LRU_HEAD_DIM, LRU_HEAD_DIM), LRU_HEAD_DIM ** -0.5),
        "lru_gate_a_b": nrm((E, LRU_WIDTH), 0.01),
        "lru_gate_x_w": nrm((E, LRU_HEADS, LRU_HEAD_DIM, LRU_HEAD_DIM), LRU_HEAD_DIM ** -0.5),
        "lru_gate_x_b": nrm((E, LRU_WIDTH), 0.01),
        "lru_lambda": lru_lambda,
        "ssd_conv_w": nrm((E, SSD_CONV, SSD_CONV_DIM), SSD_CONV ** -0.5),
        "ssd_conv_b": nrm((E, SSD_CONV_DIM), 0.01),
        "ssd_dt_bias": ssd_dt_bias,
        "ssd_a_log": ssd_a_log,
        "ssd_d": gain((E, SSD_HEADS)),
        "ssd_norm_g": gain((E, SSD_INNER)),
        "even_w_out": nrm((E, EVEN_MIX, D_MODEL), EVEN_MIX ** -0.5),
        "odd_norm_g": gain((O, D_MODEL)),
        "odd_w_in": nrm((O, D_MODEL, 3 * SC_WIDTH), D_MODEL ** -0.5),
        "odd_conv_w": nrm((O, SC_CONV, SC_WIDTH), SC_CONV ** -0.5),
        "odd_w_out": nrm((O, SC_WIDTH, D_MODEL), SC_WIDTH ** -0.5),
        "ffn_norm_g": gain((DEPTH, D_MODEL)),
        "peer_w_query": nrm((DEPTH, D_MODEL, PEER_HEADS * PEER_QDIM), D_MODEL ** -0.5),
        "peer_sub_keys": nrm((DEPTH, PEER_HEADS, 2, PEER_NKEYS, PEER_HALF), PEER_HALF ** -0.5),
        "peer_u": nrm((DEPTH, PEER_NEXPERTS, D_MODEL), D_MODEL ** -0.5),
        "peer_v": nrm((DEPTH, PEER_NEXPERTS, D_MODEL), (PEER_HEADS * PEER_TOPK) ** -0.5),
        "final_norm_g": gain((D_MODEL,)),
    }


def reference(x, even_norm_g, even_w_in, lru_conv_w, lru_conv_b, lru_gate_a_w, lru_gate_a_b,
              lru_gate_x_w, lru_gate_x_b, lru_lambda, ssd_conv_w, ssd_conv_b, ssd_dt_bias,
              ssd_a_log, ssd_d, ssd_norm_g, even_w_out, odd_norm_g, odd_w_in, odd_conv_w,
              odd_w_out, ffn_norm_g, peer_w_query, peer_sub_keys, peer_u, peer_v, final_norm_g):
    h = x
    for layer in range(DEPTH):
        i = layer // 2
        if layer % 2 == 0:
            h = h + even_mixer(rmsnorm(h, even_norm_g[i]), even_w_in[i], lru_conv_w[i], lru_conv_b[i],
                               lru_gate_a_w[i], lru_gate_a_b[i], lru_gate_x_w[i], lru_gate_x_b[i],
                               lru_lambda[i], ssd_conv_w[i], ssd_conv_b[i], ssd_dt_bias[i],
                               ssd_a_log[i], ssd_d[i], ssd_norm_g[i], even_w_out[i])
        else:
            h = h + odd_mixer(rmsnorm(h, odd_norm_g[i]), odd_w_in[i], odd_conv_w[i], odd_w_out[i])
        h = h + peer_ffn(rmsnorm(h, ffn_norm_g[layer]), peer_w_query[layer], peer_sub_keys[layer],
                         peer_u[layer], peer_v[layer])
    return rmsnorm(h, final_norm_g)
```

```python
from contextlib import ExitStack
import numpy as np
import concourse.bass as bass
import concourse.mybir as mybir
from concourse.bass_utils import run_bass_kernel_spmd

F32 = mybir.dt.float32
BF16 = mybir.dt.bfloat16
I32 = mybir.dt.int32
U32 = mybir.dt.uint32
U8 = mybir.dt.uint8
AF = mybir.ActivationFunctionType
ALU = mybir.AluOpType
AX = mybir.AxisListType

D = 1024
NCORES = 8
EVEN_IN = 4624
NEG = -1.0e30
WRITE_KW = ("out", "accum_out", "out_max", "out_indices")


def _space(ap):
    n = type(ap.tensor).__name__
    if "SB" in n:
        return "sb"
    if "PSum" in n:
        return "ps"
    return "dram"


class KB:
    NDS = 48
    SAME_RAW = True

    def __init__(self, nc):
        self.nc = nc
        self.eng = dict(pe=nc.tensor, dve=nc.vector, act=nc.scalar, pool=nc.gpsimd, sp=nc.sync)
        self.esem = {k: nc.alloc_semaphore("es_" + k) for k in self.eng}
        self.ecnt = {k: 0 for k in self.eng}
        self.seen = {k: {} for k in self.eng}
        self.dsems = [nc.alloc_semaphore("ds%d" % i) for i in range(self.NDS)]
        self.dval = [0] * self.NDS
        self.dpool = {"sp": list(range(0, 32)), "pool": list(range(32, 44)), "act": list(range(44, 48))}
        self.dnext = {k: 0 for k in self.dpool}
        self.lastw = {}
        self.rd_eng = {}
        self.rd_dma = {}
        self.nins = 0

    def _wait(self, e, tok):
        sem, val, src, sid = tok
        if self.seen[e].get(sid, 0) >= val:
            return
        self.eng[e].wait_ge(sem, val)
        self.seen[e][sid] = val
        self.nins += 1

    def _sync(self, e, reads, writes):
        for k in reads:
            t = self.lastw.get(k)
            if t is not None and not (t[2] == e and (e == "pe" or not self.SAME_RAW)):
                self._wait(e, t)
        for k in writes:
            t = self.lastw.get(k)
            if t is not None and t[2] != e:
                self._wait(e, t)
            for src, t in self.rd_eng.get(k, {}).items():
                if src != e:
                    self._wait(e, t)
            for t in self.rd_dma.get(k, ()):
                self._wait(e, t)

    def _record(self, tok, reads, writes):
        for k in writes:
            self.lastw[k] = tok
            self.rd_eng[k] = {}
            self.rd_dma[k] = []
        for k in reads:
            if k in writes:
                continue
            if tok[2] is None:
                self.rd_dma.setdefault(k, []).append(tok)
            else:
                self.rd_eng.setdefault(k, {})[tok[2]] = tok

    def op(self, e, method, rk=(), wk=(), xr=None, xw=None, **kw):
        reads, writes = set(rk), set(wk)
        for k, v in kw.items():
            if isinstance(v, bass.AP):
                (writes if k in WRITE_KW else reads).add(v.tensor.name)
        if xr is not None:
            reads = set(xr)
        if xw is not None:
            writes = set(xw)
        self._sync(e, reads, writes)
        ins = getattr(self.eng[e], method)(**kw)
        self.ecnt[e] += 1
        ins.then_inc(self.esem[e], 1)
        self.nins += 1
        tok = (self.esem[e], self.ecnt[e], e, "e_" + e)
        self._record(tok, reads, writes)
        return ins

    def dma(self, q, out, in_, rk=(), wk=()):
        reads, writes = set(rk), set(wk)
        if _space(out) != "dram":
            writes.add(out.tensor.name)
        if _space(in_) != "dram":
            reads.add(in_.tensor.name)
        self._sync(q, reads, writes)
        pl = self.dpool[q]
        i = pl[self.dnext[q] % len(pl)]
        self.dnext[q] += 1
        sid = "d_%d" % i
        if self.dval[i] > 0:
            self._wait(q, (self.dsems[i], self.dval[i], None, sid))
        self.dval[i] += 16
        self.eng[q].dma_start(out=out, in_=in_).then_inc(self.dsems[i], 16)
        self.nins += 1
        tok = (self.dsems[i], self.dval[i], None, sid)
        self._record(tok, reads, writes)

    def barrier(self):
        for e in self.eng:
            for e2 in self.eng:
                if e2 != e and self.ecnt[e2] > 0:
                    self._wait(e, (self.esem[e2], self.ecnt[e2], e2, "e_" + e2))
            for i in range(self.NDS):
                if self.dval[i] > 0:
                    self._wait(e, (self.dsems[i], self.dval[i], None, "d_%d" % i))
        self.lastw.clear()
        self.rd_eng.clear()
        self.rd_dma.clear()

    def mm(self, out, lhsT, rhs, start, stop):
        return self.op("pe", "matmul", out=out, lhsT=lhsT, rhs=rhs, start=start, stop=stop)

    def tr(self, out, in_, ident):
        return self.op("pe", "transpose", out=out, in_=in_, identity=ident)

    def act(self, out, in_, func, **kw):
        return self.op("act", "activation", out=out, in_=in_, func=func, **kw)

    def tt(self, e, out, in0, in1, op):
        return self.op(e, "tensor_tensor", out=out, in0=in0, in1=in1, op=op)

    def ts(self, e, out, in0, s1, op0, s2=None, op1=None, **kw):
        if op1 is None:
            return self.op(e, "tensor_scalar", out=out, in0=in0, scalar1=s1, scalar2=None, op0=op0, **kw)
        return self.op(e, "tensor_scalar", out=out, in0=in0, scalar1=s1, scalar2=s2, op0=op0, op1=op1, **kw)

    def stt(self, out, in0, scalar, in1, op0, op1):
        return self.op("dve", "scalar_tensor_tensor", out=out, in0=in0, scalar=scalar, in1=in1, op0=op0, op1=op1)

    def copy(self, e, out, in_):
        if e == "act":
            return self.op("act", "activation", out=out, in_=in_, func=AF.Copy)
        return self.op(e, "tensor_copy", out=out, in_=in_)


class Ctx:
    pass


def bc(ap, shape):
    return ap.to_broadcast(list(shape))


def setup_consts(c):
    nc, kb = c.nc, c.kb
    A = nc.alloc_sbuf_tensor
    c.iota_i = A("c_iota_i", [128, 128], I32)
    c.part_i = A("c_part_i", [128, 1], I32)
    c.iota_f = A("c_iota_f", [128, 128], F32)
    c.iota_b = A("c_iota_b", [128, 128], BF16)
    c.part_f = A("c_part_f", [128, 1], F32)
    c.ident_f = A("c_ident_f", [128, 128], F32)
    c.ident_b = A("c_ident_b", [128, 128], BF16)
    c.triU = A("c_triU", [128, 128], F32)
    c.maskgt = A("c_maskgt", [128, 128], F32)
    c.ones_f = A("c_ones_f", [128, 128], F32)
    c.junk_b = A("c_junk_b", [128, 1024], BF16)
    c.eps_t = A("c_eps_t", [128, 1], F32)
    kb.op("dve", "memset", ap=c.eps_t[:], constant=1e-6, wk=("c_eps_t",))
    kb.op("pool", "iota", out=c.iota_i[:], pattern=[[1, 128]], base=0, channel_multiplier=0)
    kb.op("pool", "iota", out=c.part_i[:], pattern=[[0, 1]], base=0, channel_multiplier=1)
    kb.copy("dve", c.iota_f[:], c.iota_i[:])
    kb.copy("dve", c.iota_b[:], c.iota_i[:])
    kb.copy("dve", c.part_f[:], c.part_i[:])
    kb.ts("dve", c.ident_f[:], c.iota_f[:], c.part_f[:, 0:1], ALU.is_equal)
    kb.copy("dve", c.ident_b[:], c.ident_f[:])
    kb.ts("dve", c.triU[:], c.iota_f[:], c.part_f[:, 0:1], ALU.is_ge)
    kb.ts("dve", c.maskgt[:], c.iota_f[:], c.part_f[:, 0:1], ALU.is_lt)
    kb.op("dve", "memset", ap=c.ones_f[:], constant=1.0, wk=("c_ones_f",))


def rmsnorm_tile(c, ht, gbc, xn, ssq, rstd):
    kb = c.kb
    kb.act(c.junk_b[:], ht, AF.Square, accum_out=ssq)
    kb.act(rstd, ssq, AF.Ln, scale=1.0 / D, bias=c.eps_t[:, 0:1])
    kb.act(rstd, rstd, AF.Exp, scale=-0.5)
    kb.stt(xn, ht, rstd, gbc, ALU.mult, ALU.mult)


def transpose_to(c, src_b, dstT, psT, n=8):
    kb = c.kb
    for k in range(n):
        kb.tr(psT[:, k * 128:(k + 1) * 128], src_b[:, k * 128:(k + 1) * 128], c.ident_b[:])
    kb.copy("act", dstT, psT[:, 0:n * 128].rearrange("p (k t) -> p k t", k=n))


def load_w_bf16(c, dst, src2d, ndk):
    for dk in range(ndk):
        c.kb.dma("pool", dst[:, dk, :], src2d[dk * 128:(dk + 1) * 128, :])


def phase_odd(c, i, src, src_key):
    nc, kb, S = c.nc, c.kb, c.S
    TS = min(512, S)
    NT = TS // 128
    NSC = S // TS
    with ExitStack() as es:
        def SB(name, shape, dt):
            return es.enter_context(nc.sbuf_tensor(c.pfx + "o_" + name, shape, dt))

        def PS(name, shape, dt=F32):
            return es.enter_context(nc.psum_tensor(c.pfx + "o_" + name, shape, dt))
        wi = SB("wi", [128, 8, 3072], BF16)
        wo = SB("wo", [128, 8, 1024], BF16)
        cw = SB("cw", [128, 8, 3], F32)
        gbc = SB("g", [128, 1024], F32)
        hres = [SB("h%d" % q, [128, NT, 1024], F32) for q in range(2)]
        xnT = [SB("xnT%d" % q, [128, 8, TS], BF16) for q in range(2)]
        uT = [SB("uT%d" % q, [128, 8, TS], BF16) for q in range(2)]
        xn = [SB("xn%d" % q, [128, 1024], BF16) for q in range(2)]
        csb = [SB("csb%d" % q, [128, TS], F32) for q in range(2)]
        acc = [SB("acc%d" % q, [128, TS], F32) for q in range(2)]
        hout = [SB("ho%d" % q, [128, 1024], F32) for q in range(2)]
        st = [SB("st%d" % q, [128, 4], F32) for q in range(2)]
        cvb = [SB("cvb%d" % f, [128, TS + 2], F32) for f in range(8)]
        po = PS("po", [128, 1024])
        psT = po[:, 0:512].bitcast(BF16)
        pb = [PS("pb%d" % q, [128, 512]) for q in range(2)]
        pc = [PS("pc%d" % q, [128, 512]) for q in range(2)]
        pv = [PS("pv%d" % q, [128, 512]) for q in range(2)]
        load_w_bf16(c, wi, c.d["odd_w_in"][i], 8)
        load_w_bf16(c, wo, c.d["odd_w_out"][i], 8)
        kb.dma("sp", cw[:], c.d["ocw"][i])
        kb.dma("sp", gbc[:], c.d["odd_norm_g"][i, :].partition_broadcast(128))
        c.after_weights()
        for f in range(8):
            kb.op("dve", "memset", ap=cvb[f][:, 0:2], constant=0.0, wk=(cvb[f].name,))

        def front_tile(sc, tt):
            q = sc % 2
            r0 = sc * TS + tt * 128
            kb.dma("sp", hres[q][:, tt, :], src[r0:r0 + 128, :], rk=((src_key, r0 // 128),))
            rmsnorm_tile(c, hres[q][:, tt, :], gbc[:], xn[tt % 2][:], st[tt % 2][:, 0:1], st[tt % 2][:, 1:2])
            for k in range(8):
                kb.tr(psT[:, k * 128:(k + 1) * 128], xn[tt % 2][:, k * 128:(k + 1) * 128], c.ident_b[:])
            kb.copy("act", xnT[q][:, :, tt * 128:(tt + 1) * 128], psT.rearrange("p (k t) -> p k t", k=8))

        def mid(sc, f):
            q = sc % 2
            p_ = f % 2
            for (pp, off) in ((pb[p_], 0), (pc[p_], 1024), (pv[p_], 2048)):
                for dk in range(8):
                    kb.mm(pp[:, 0:TS], wi[:, dk, off + f * 128: off + (f + 1) * 128], xnT[q][:, dk, :], dk == 0, dk == 7)
            kb.copy("act", csb[p_][:], pc[p_][:, 0:TS])
            kb.tt("dve", cvb[f][:, 2:TS + 2], pv[p_][:, 0:TS], csb[p_][:], ALU.mult)
            kb.ts("dve", acc[p_][:], cvb[f][:, 0:TS], cw[:, f, 0:1], ALU.mult)
            kb.stt(acc[p_][:], cvb[f][:, 1:TS + 1], cw[:, f, 1:2], acc[p_][:], ALU.mult, ALU.add)
            kb.stt(acc[p_][:], cvb[f][:, 2:TS + 2], cw[:, f, 2:3], acc[p_][:], ALU.mult, ALU.add)
            kb.tt("dve", uT[q][:, f, :], pb[p_][:, 0:TS], acc[p_][:], ALU.mult)
            kb.copy("act", cvb[f][:, 0:2], cvb[f][:, TS:TS + 2])

        def back(sc):
            q = sc % 2
            for tt in range(NT):
                r0 = sc * TS + tt * 128
                for half in range(2):
                    for kc in range(8):
                        kb.mm(po[:, half * 512:(half + 1) * 512], uT[q][:, kc, tt * 128:(tt + 1) * 128],
                              wo[:, kc, half * 512:(half + 1) * 512], kc == 0, kc == 7)
                kb.tt("dve", hout[tt % 2][:], po[:], hres[q][:, tt, :], ALU.add)
                kb.dma("sp", c.hbuf[r0:r0 + 128, :], hout[tt % 2][:], wk=(("h", r0 // 128),))

        for tt in range(NT):
            front_tile(0, tt)
        for sc in range(NSC):
            for f in range(8):
                mid(sc, f)
                if f == 3:
                    if sc > 0:
                        back(sc - 1)
                if f >= 4 and sc + 1 < NSC and (f - 4) < NT:
                    front_tile(sc + 1, f - 4)
        back(NSC - 1)
        kb.barrier()


def phase_peer_route(c, l, rt):
    nc, kb, S = c.nc, c.kb, c.S
    iT, jT, gT = rt
    with ExitStack() as es:
        wq = es.enter_context(nc.sbuf_tensor(c.pfx + "r_wq", [128, 8, 2048], BF16))
        kn = es.enter_context(nc.sbuf_tensor(c.pfx + "r_kn", [128, 16, 128], F32))
        kT = es.enter_context(nc.sbuf_tensor(c.pfx + "r_kT", [128, 16, 128], BF16))
        gbc = es.enter_context(nc.sbuf_tensor(c.pfx + "r_g", [128, 1024], F32))
        ht_2 = [es.enter_context(nc.sbuf_tensor(c.pfx + "r_h%d" % q, [128, 1024], F32)) for q in range(2)]
        xn_2 = [es.enter_context(nc.sbuf_tensor(c.pfx + "r_xn%d" % q, [128, 1024], BF16)) for q in range(2)]
        xnT_2 = [es.enter_context(nc.sbuf_tensor(c.pfx + "r_xnT%d" % q, [128, 8, 128], BF16)) for q in range(2)]
        qT_2 = [es.enter_context(nc.sbuf_tensor(c.pfx + "r_qT%d" % q, [128, 16, 128], BF16)) for q in range(2)]
        sc_2 = [es.enter_context(nc.sbuf_tensor(c.pfx + "r_sc%d" % q, [128, 2048], F32)) for q in range(2)]
        wk_ = es.enter_context(nc.sbuf_tensor(c.pfx + "r_wk", [128, 2048], F32))
        sv = es.enter_context(nc.sbuf_tensor(c.pfx + "r_sv", [128, 16, 16], F32))
        si = es.enter_context(nc.sbuf_tensor(c.pfx + "r_si", [128, 16, 16], U32))
        sif = es.enter_context(nc.sbuf_tensor(c.pfx + "r_sif", [128, 16, 16], F32))
        cand = es.enter_context(nc.sbuf_tensor(c.pfx + "r_cand", [128, 8, 112], F32))
        cwk = es.enter_context(nc.sbuf_tensor(c.pfx + "r_cwk", [128, 8, 112], F32))
        hif, lof, mB, dd, ee = [es.enter_context(nc.sbuf_tensor(c.pfx + "r_" + n_, [128, 8, 16], F32))
                                for n_ in ("hif", "lof", "mB", "dd", "ee")]
        tv = es.enter_context(nc.sbuf_tensor(c.pfx + "r_tv", [128, 8, 16], F32))
        tp = es.enter_context(nc.sbuf_tensor(c.pfx + "r_tp", [128, 8, 16], U32))
        k1u = es.enter_context(nc.sbuf_tensor(c.pfx + "r_k1", [128, 8, 16], U32))
        k2u = es.enter_context(nc.sbuf_tensor(c.pfx + "r_k2", [128, 8, 16], U32))
        k1f = es.enter_context(nc.sbuf_tensor(c.pfx + "r_k1f", [128, 8, 16], F32))
        k2f = es.enter_context(nc.sbuf_tensor(c.pfx + "r_k2f", [128, 8, 16], F32))
        oh = es.enter_context(nc.sbuf_tensor(c.pfx + "r_oh", [128, 8, 16, 16], BF16))
        oh2 = es.enter_context(nc.sbuf_tensor(c.pfx + "r_oh2", [128, 8, 16, 16], BF16))
        i_f = es.enter_context(nc.sbuf_tensor(c.pfx + "r_if", [128, 128], F32))
        j_f = es.enter_context(nc.sbuf_tensor(c.pfx + "r_jf", [128, 128], F32))
        g_f = es.enter_context(nc.sbuf_tensor(c.pfx + "r_gf", [128, 128], F32))
        sm = es.enter_context(nc.sbuf_tensor(c.pfx + "r_sm", [128, 8, 4], F32))
        st2 = [es.enter_context(nc.sbuf_tensor(c.pfx + "r_st%d" % q, [128, 4], F32)) for q in range(2)]
        psT = es.enter_context(nc.psum_tensor(c.pfx + "r_psT", [128, 1024], BF16))
        pq = es.enter_context(nc.psum_tensor(c.pfx + "r_pq", [128, 2048], F32))
        ptr = es.enter_context(nc.psum_tensor(c.pfx + "r_ptr", [128, 512], F32))
        load_w_bf16(c, wq, c.d["peer_w_query"][l], 8)
        c.after_weights()
        kb.dma("sp", kn[:], c.d["peer_sub_keys"][l].rearrange("h p n d -> n (h p) d"))
        kb.dma("sp", gbc[:], c.d["ffn_norm_g"][l, :].partition_broadcast(128))
        for hp in range(16):
            kb.tr(ptr[:, (hp % 4) * 128:(hp % 4 + 1) * 128], kn[:, hp, :], c.ident_f[:])
            if hp % 4 == 3:
                kb.copy("act", kT[:, hp - 3:hp + 1, :], ptr[:].rearrange("p (k t) -> p k t", k=4))
        sv4 = sv[:].rearrange("p (h two) k -> p h two k", two=2)
        sif4 = sif[:].rearrange("p (h two) k -> p h two k", two=2)
        iota16 = c.iota_f[:, 0:16]
        def front(tt):
            r0 = tt * 128
            ht, xn, xnT, qT, sc = ht_2[tt % 2], xn_2[tt % 2], xnT_2[tt % 2], qT_2[tt % 2], sc_2[tt % 2]
            kb.dma("sp", ht[:], c.hbuf[r0:r0 + 128, :], rk=(("h", tt),))
            rmsnorm_tile(c, ht[:], gbc[:], xn[:], st2[tt % 2][:, 0:1], st2[tt % 2][:, 1:2])
            transpose_to(c, xn, xnT[:], psT)
            blk, off = tt // 2, (tt % 2) * 128
            kb.dma("sp", c.xnTd[blk, :, :, off:off + 128], xnT[:], wk=(("xnT", blk, tt % 2),))
            for hp in range(16):
                for dk in range(8):
                    kb.mm(pq[:, hp * 128:(hp + 1) * 128], wq[:, dk, hp * 128:(hp + 1) * 128], xnT[:, dk, :], dk == 0, dk == 7)
            kb.copy("act", qT[:], pq[:].rearrange("p (k t) -> p k t", k=16))
            for hp in range(16):
                kb.mm(pq[:, hp * 128:(hp + 1) * 128], qT[:, hp, :], kT[:, hp, :], True, True)
            kb.copy("act", sc[:], pq[:])

        def back(tt):
            r0 = tt * 128
            sc = sc_2[tt % 2]
            SC = sc.name
            S_ = lambda hp: sc[:, hp * 128:(hp + 1) * 128]
            W_ = lambda hp: wk_[:, hp * 128:(hp + 1) * 128]
            for hp in range(16):
                kb.op("dve", "max", out=sv[:, hp, 0:8], in_=S_(hp), xr={SC}, xw={("sv", hp, 0)})
            for hp in range(16):
                kb.op("dve", "max_index", out=si[:, hp, 0:8], in_max=sv[:, hp, 0:8], in_values=S_(hp),
                      xr={SC, ("sv", hp, 0)}, xw={("si", hp, 0)})
            for hp in range(16):
                kb.op("dve", "match_replace", out=W_(hp), in_to_replace=sv[:, hp, 0:8], in_values=S_(hp), imm_value=NEG,
                      xr={SC, ("sv", hp, 0)}, xw={("wk", hp)})
            for hp in range(16):
                kb.op("dve", "max", out=sv[:, hp, 8:16], in_=W_(hp), xr={("wk", hp)}, xw={("sv", hp, 1)})
            for hp in range(16):
                kb.op("dve", "max_index", out=si[:, hp, 8:16], in_max=sv[:, hp, 8:16], in_values=W_(hp),
                      xr={("wk", hp), ("sv", hp, 1)}, xw={("si", hp, 1)})
            SV_ALL = {("sv", hp, q) for hp in range(16) for q in range(2)}
            SI_ALL = {("si", hp, q) for hp in range(16) for q in range(2)}
            kb.op("dve", "tensor_copy", out=sif[:], in_=si[:], xr=SI_ALL, xw={sif.name})
            candA = cand[:, :, 0:64].rearrange("p h (a b) -> p h a b", a=16)
            candB = cand[:, :, 64:112].rearrange("p h (a b) -> p h a b", a=12)
            kb.op("dve", "tensor_tensor", out=candA, in0=bc(sv4[:, :, 0, :].unsqueeze(3), [128, 8, 16, 4]),
                  in1=bc(sv4[:, :, 1, 0:4].unsqueeze(2), [128, 8, 16, 4]), op=ALU.add, xr=SV_ALL, xw={cand.name})
            kb.op("dve", "tensor_tensor", out=candB, in0=bc(sv4[:, :, 1, 4:16].unsqueeze(3), [128, 8, 12, 4]),
                  in1=bc(sv4[:, :, 0, 0:4].unsqueeze(2), [128, 8, 12, 4]), op=ALU.add, xr=SV_ALL, xw={cand.name})
            CD = cand.name
            for h in range(8):
                kb.op("dve", "max", out=tv[:, h, 0:8], in_=cand[:, h, :], xr={CD}, xw={("tv", h, 0)})
            for h in range(8):
                kb.op("dve", "max_index", out=tp[:, h, 0:8], in_max=tv[:, h, 0:8], in_values=cand[:, h, :],
                      xr={CD, ("tv", h, 0)}, xw={("tp", h, 0)})
            for h in range(8):
                kb.op("dve", "match_replace", out=cwk[:, h, :], in_to_replace=tv[:, h, 0:8], in_values=cand[:, h, :],
                      imm_value=NEG, xr={CD, ("tv", h, 0)}, xw={("cwk", h)})
            for h in range(8):
                kb.op("dve", "max", out=tv[:, h, 8:16], in_=cwk[:, h, :], xr={("cwk", h)}, xw={("tv", h, 1)})
            for h in range(8):
                kb.op("dve", "max_index", out=tp[:, h, 8:16], in_max=tv[:, h, 8:16], in_values=cwk[:, h, :],
                      xr={("cwk", h), ("tv", h, 1)}, xw={("tp", h, 1)})
            TV_ALL = {("tv", h, q) for h in range(8) for q in range(2)}
            TP_ALL = {("tp", h, q) for h in range(8) for q in range(2)}
            kb.op("dve", "tensor_single_scalar", out=k1u[:], in_=tp[:], scalar=2, op=ALU.logical_shift_right,
                  xr=TP_ALL, xw={k1u.name})
            kb.op("dve", "tensor_single_scalar", out=k2u[:], in_=tp[:], scalar=3, op=ALU.bitwise_and,
                  xr=TP_ALL, xw={k2u.name})
            i3 = i_f[:].rearrange("p (h k) -> p h k", h=8)
            j3 = j_f[:].rearrange("p (h k) -> p h k", h=8)
            g3 = g_f[:].rearrange("p (h k) -> p h k", h=8)
            kb.op("dve", "tensor_tensor", out=g3, in0=tv[:], in1=bc(tv[:, :, 0:1], [128, 8, 16]), op=ALU.subtract,
                  xr=TV_ALL, xw={g_f.name})
            kb.copy("dve", hif[:], k1u[:])
            kb.copy("dve", lof[:], k2u[:])
            kb.act(g3, g3, AF.Exp)
            kb.ts("dve", mB[:], hif[:], 16.0, ALU.is_ge)
            kb.tt("dve", dd[:], lof[:], hif[:], ALU.subtract)
            kb.ts("dve", ee[:], dd[:], -1.0, ALU.mult, -12.0, ALU.add)
            kb.tt("dve", dd[:], dd[:], mB[:], ALU.mult)
            kb.tt("dve", ee[:], ee[:], mB[:], ALU.mult)
            kb.tt("dve", k1f[:], dd[:], hif[:], ALU.add)
            kb.tt("dve", k2f[:], ee[:], lof[:], ALU.add)
            io4 = bc(iota16.unsqueeze(1).unsqueeze(1), [128, 8, 16, 16])
            kb.tt("dve", oh[:], io4, bc(k1f[:].unsqueeze(3), [128, 8, 16, 16]), ALU.is_equal)
            kb.tt("dve", oh2[:], io4, bc(k2f[:].unsqueeze(3), [128, 8, 16, 16]), ALU.is_equal)
            kb.tt("dve", oh[:], oh[:], bc(sif4[:, :, 0, :].unsqueeze(2), [128, 8, 16, 16]), ALU.mult)
            kb.tt("dve", oh2[:], oh2[:], bc(sif4[:, :, 1, :].unsqueeze(2), [128, 8, 16, 16]), ALU.mult)
            kb.op("dve", "tensor_reduce", out=sm[:, :, 0], in_=g3, axis=AX.X, op=ALU.add)
            kb.op("dve", "tensor_reduce", out=i3, in_=oh[:], axis=AX.X, op=ALU.add)
            kb.op("dve", "tensor_reduce", out=j3, in_=oh2[:], axis=AX.X, op=ALU.add)
            kb.op("dve", "reciprocal", out=sm[:, :, 1], in_=sm[:, :, 0])
            kb.tt("dve", g3, g3, bc(sm[:, :, 1:2], [128, 8, 16]), ALU.mult)
            for n_, (srcf, dstT) in enumerate(((i_f, iT), (j_f, jT), (g_f, gT))):
                kb.tr(ptr[:, n_ * 128:(n_ + 1) * 128], srcf[:], c.ident_f[:])
                kb.copy("act", dstT[:, r0:r0 + 128], ptr[:, n_ * 128:(n_ + 1) * 128])

        NTL = S // 128
        front(0)
        for tt in range(NTL):
            if tt + 1 < NTL:
                front(tt + 1)
            back(tt)
        kb.barrier()


def phase_peer_experts(c, l, rt, dst, dst_key, final_g=None):
    nc, kb, S = c.nc, c.kb, c.S
    iT, jT, gT = rt
    TB = 256
    NB = 3
    NPA = 3
    PD = 2
    NBLK = S // TB
    with ExitStack() as es:
        def SB(name, shape, dt):
            return es.enter_context(nc.sbuf_tensor(c.pfx + "e_" + name, shape, dt))

        def PS(name, shape, dt=F32):
            return es.enter_context(nc.psum_tensor(c.pfx + "e_" + name, shape, dt))
        G2 = [SB("G%d" % q, [128, TB, 128], BF16) for q in range(2)]
        xb = SB("xb", [128, 8, TB], BF16)
        As = [SB("A%d" % q, [128, 4, 128], BF16) for q in range(2)]
        Bs = [SB("B%d" % q, [128, 4, 128], BF16) for q in range(2)]
        gas = [SB("ga%d" % q, [128, TB], F32) for q in range(NPA)]
        was = [SB("wa%d" % q, [128, TB], BF16) for q in range(NPA)]
        ht = SB("h", [128, 1024], F32)
        houts = [SB("ho0", [128, 1024], F32), SB("ho1", [128, 1024], F32)]
        st = SB("st", [128, 4], F32)
        ut = [SB("ut%d" % q, [128, 2, 1024], BF16) for q in range(NB)]
        vt = [SB("vt%d" % q, [128, 2, 1024], BF16) for q in range(NB)]
        pG = [PS("pG%d" % q, [128, 512]) for q in range(2)]
        pA = [PS("pA%d" % q, [128, 512]) for q in range(2)]
        pO = [PS("pO%d" % q, [128, 1024]) for q in range(2)]
        if final_g is not None:
            gfin = SB("gf", [128, 1024], F32)
            kb.dma("sp", gfin[:], final_g.partition_broadcast(128))

        def g_dve(bk, u):
            A_, B_ = As[u % 2], Bs[u % 2]
            for t in range(4):
                tg = bk * TB + 4 * u + t
                kb.ts("dve", A_[:, t, :], c.iota_b[:], iT[:, tg:tg + 1], ALU.is_equal, gT[:, tg:tg + 1], ALU.mult)
                kb.ts("dve", B_[:, t, :], c.iota_b[:], jT[:, tg:tg + 1], ALU.is_equal)

        def g_pe(bk, u):
            A_, B_ = As[u % 2], Bs[u % 2]
            for t in range(4):
                kb.mm(pG[u % 2][:, t * 128:(t + 1) * 128], B_[:, t, :], A_[:, t, :], True, True)

        def g_evac(bk, u):
            kb.copy("act", G2[bk % 2][:, 4 * u:4 * u + 4, :], pG[u % 2][:].rearrange("p (u i) -> p u i", u=4))

        NU = TB // 4

        def g_slot(bk, s_):
            if bk >= NBLK:
                return
            if 0 <= s_ - 2 < NU:
                g_evac(bk, s_ - 2)
            if 0 <= s_ - 1 < NU:
                g_pe(bk, s_ - 1)
            if 0 <= s_ < NU:
                g_dve(bk, s_)

        for s_ in range(NU + 2):
            g_slot(0, s_)
        pending = None
        for b in range(NBLK):
            t0 = b * TB
            G = G2[b % 2]
            kb.dma("sp", xb[:], c.xnTd[b], rk=(("xnT", b, 0), ("xnT", b, 1)))

            def issue_act(ch):
                cp, u = ch // 2, ch % 2
                bu = ut[cp % NB]
                if u == 0:
                    kb.dma("sp", bu[:], c.utb[l, cp], rk=tuple(("utb", l, k) for k in range(32)))
                    kb.dma("sp", vt[cp % NB][:], c.vb[l, cp], rk=tuple(("vb", l, k) for k in range(32)))
                pa = pA[(ch // 2) % 2][:, (ch % 2) * TB:(ch % 2 + 1) * TB]
                for dk in range(8):
                    kb.mm(pa, bu[:, u, dk * 128:(dk + 1) * 128], xb[:, dk, :], dk == 0, dk == 7)
                kb.act(gas[ch % NPA][:], pa, AF.Gelu_apprx_tanh)
                kb.tt("dve", was[ch % NPA][:], gas[ch % NPA][:], G[:, :, ch], ALU.mult)

            def issue_v(ch):
                cp, u = ch // 2, ch % 2
                bv = vt[cp % NB]
                w_ = was[ch % NPA]
                for tt in range(2):
                    for half in range(2):
                        kb.mm(pO[tt][:, half * 512:(half + 1) * 512], w_[:, tt * 128:(tt + 1) * 128],
                              bv[:, u, half * 512:(half + 1) * 512], ch == 0, ch == 127)
            for ch in range(PD):
                issue_act(ch)
            if pending is not None:
                pending()
            for ch in range(128):
                if ch + PD < 128:
                    issue_act(ch + PD)
                issue_v(ch)
                if ch % 2 == 1:
                    g_slot(b + 1, ch // 2)
            g_slot(b + 1, NU)
            g_slot(b + 1, NU + 1)

            def epilogue(t0=t0):
                for tt in range(2):
                    r0 = t0 + tt * 128
                    kb.dma("sp", ht[:], c.hbuf[r0:r0 + 128, :], rk=(("h", r0 // 128),))
                    kb.tt("dve", houts[tt][:], pO[tt][:], ht[:], ALU.add)
                    if final_g is None:
                        kb.dma("sp", dst[r0:r0 + 128, :], houts[tt][:], wk=((dst_key, r0 // 128),))
                    else:
                        rmsnorm_tile(c, houts[tt][:], gfin[:], ht[:], st[:, 0:1], st[:, 1:2])
                        kb.dma("sp", dst[r0:r0 + 128, :], ht[:], wk=((dst_key, r0 // 128),))
            pending = epilogue
        pending()
        kb.barrier()


IN_SPECS = [
    ("x", None, F32),
    ("even_norm_g", [2, 1024], F32), ("even_w_in", [2, 1024, EVEN_IN], F32),
    ("lcw", [2, 128, 8, 4], F32), ("lcb", [2, 128, 8], F32),
    ("lru_gate_a_w", [2, 8, 128, 128], F32), ("gab", [2, 128, 8], F32),
    ("lru_gate_x_w", [2, 8, 128, 128], F32), ("gxb", [2, 128, 8], F32),
    ("lam", [2, 128, 8], F32),
    ("scw", [2, 128, 12, 4], F32), ("scb", [2, 128, 12], F32),
    ("ssd_dt_bias", [2, 16], F32), ("ssd_a_log", [2, 16], F32), ("ssd_d", [2, 16], F32),
    ("ssd_norm_g", [2, 1024], F32), ("even_w_out", [2, 2048, 1024], F32),
    ("odd_norm_g", [2, 1024], F32), ("odd_w_in", [2, 1024, 3072], F32), ("ocw", [2, 128, 8, 3], F32),
    ("odd_w_out", [2, 1024, 1024], F32),
    ("ffn_norm_g", [4, 1024], F32), ("peer_w_query", [4, 1024, 2048], F32),
    ("peer_sub_keys", [4, 8, 2, 128, 128], F32),
    ("peer_ut", [4, 128, 128, 1024], F32), ("peer_v", [4, 16384, 1024], F32),
    ("final_norm_g", [1, 1024], F32),
]


def build_program(S, plan, nlay=4):
    nc = bass.Bass("TRN2", target_bir_lowering=False)
    c = Ctx()
    c.nc, c.S = nc, S
    c.kb = KB(nc)
    c.d = {}
    for name, shape, dt in IN_SPECS:
        if name == "x":
            shape = [S, D]
        shape = list(shape)
        if name in ("peer_w_query", "peer_sub_keys", "peer_ut", "peer_v", "ffn_norm_g"):
            shape[0] = nlay
        c.d[name] = nc.dram_tensor(name, shape, dt, kind="ExternalInput").ap()
    c.y = nc.dram_tensor("y", [S, D], F32, kind="ExternalOutput").ap()
    c.hbuf = nc.dram_tensor("hbuf", [S, D], F32, kind="Internal").ap()
    c.xnTd = nc.dram_tensor("xnTd", [max(S // 256, 1), 128, 8, 256], BF16, kind="Internal").ap()
    c.utb = nc.dram_tensor("utb", [nlay, 64, 128, 2, 1024], BF16, kind="Internal").ap()
    c.vb = nc.dram_tensor("vb", [nlay, 64, 128, 2, 1024], BF16, kind="Internal").ap()
    c.yaTd = nc.dram_tensor("yaTd", [max(S // 256, 1), 128, 8, 256], BF16, kind="Internal").ap()
    setup_consts(c)
    c.kb.barrier()
    src, src_key = c.d["x"], "x"
    converted = set()

    def make_hook(n):
        def hook():
            for st_ in plan[n + 1:]:
                if st_[0] == "peer":
                    if st_[1] not in converted:
                        converted.add(st_[1])
                        convert_experts(c, st_[1])
                    break
        return hook
    for n, step in enumerate(plan):
        last = n == len(plan) - 1
        c.pfx = "p%d_" % n
        c.after_weights = make_hook(n)
        if step[0] == "peer" and step[1] not in converted:
            converted.add(step[1])
            convert_experts(c, step[1])
            c.kb.barrier()
        kind = step[0]
        if kind == "odd":
            phase_odd(c, step[1], src, src_key)
            src, src_key = c.hbuf, "h"
        elif kind == "even":
            phase_even_lru(c, step[1], src, src_key)
            phase_even_ssd(c, step[1], src, src_key)
            src, src_key = c.hbuf, "h"
        elif kind == "copy":
            phase_copy(c, src)
            src, src_key = c.hbuf, "h"
        elif kind == "peer":
            l = step[1]
            with ExitStack() as es:
                rt = tuple(es.enter_context(nc.sbuf_tensor(c.pfx + "rt_" + nm, [128, S], dt_))
                           for nm, dt_ in (("i", BF16), ("j", U8), ("g", BF16)))
                phase_peer_route(c, l, rt)
                if last:
                    fin = c.d["final_norm_g"][0, :] if (len(step) > 2 and step[2] == "final") else None
                    phase_peer_experts(c, l, rt, c.y, "y", final_g=fin)
                else:
                    phase_peer_experts(c, l, rt, c.hbuf, "h")
        if last and kind != "peer":
            phase_out(c)
    return nc


def convert_experts(c, l):
    kb = c.kb
    for pr in range(64):
        k = pr // 2
        kb.dma("pool", c.utb[l, pr].rearrange("p u n -> u p n"), c.d["peer_ut"][l, 2 * pr:2 * pr + 2], wk=(("utb", l, k),))
        kb.dma("pool", c.vb[l, pr].rearrange("p u n -> u p n"),
               c.d["peer_v"][l, 256 * pr:256 * (pr + 1), :].rearrange("(u p) n -> u p n", u=2), wk=(("vb", l, k),))


def phase_copy(c, src):
    nc, kb, S = c.nc, c.kb, c.S
    with nc.sbuf_tensor(c.pfx + "cp_t", [128, 1024], F32) as t:
        c.after_weights()
        for tt in range(S // 128):
            kb.dma("sp", t[:], src[tt * 128:(tt + 1) * 128, :])
            kb.dma("sp", c.hbuf[tt * 128:(tt + 1) * 128, :], t[:], wk=(("h", tt),))
        kb.barrier()


def phase_out(c):
    nc, kb, S = c.nc, c.kb, c.S
    with nc.sbuf_tensor(c.pfx + "out_t", [128, 1024], F32) as t:
        for tt in range(S // 128):
            kb.dma("sp", t[:], c.hbuf[tt * 128:(tt + 1) * 128, :], rk=(("h", tt),))
            kb.dma("sp", c.y[tt * 128:(tt + 1) * 128, :], t[:], wk=(("y", tt),))
        kb.barrier()


def phase_even(c, i, src, src_key):
    raise NotImplementedError


def phase_even_lru(c, i, src, src_key):
    nc, kb, S = c.nc, c.kb, c.S
    TS = 512
    NT = TS // 128
    with ExitStack() as es:
        def SB(name, shape, dt):
            return es.enter_context(nc.sbuf_tensor(c.pfx + "l_" + name, shape, dt))

        def PS(name, shape, dt=F32):
            return es.enter_context(nc.psum_tensor(c.pfx + "l_" + name, shape, dt))
        wi = SB("wi", [128, 8, 2048], BF16)
        gaw = SB("gaw", [128, 8, 128], BF16)
        gxw = SB("gxw", [128, 8, 128], BF16)
        lcw = SB("lcw", [128, 8, 4], F32)
        lcb = SB("lcb", [128, 8], F32)
        gab = SB("gab", [128, 8], F32)
        gxb = SB("gxb", [128, 8], F32)
        lam = SB("lam", [128, 8], F32)
        c1 = SB("c1", [128, 8], F32)
        c2 = SB("c2", [128, 8], F32)
        hl = SB("hl", [128, 8], F32)
        gbc = SB("g", [128, 1024], F32)
        ht2 = [SB("h%d" % q, [128, 1024], F32) for q in range(2)]
        xn = SB("xn", [128, 1024], BF16)
        xnT2 = [SB("xnT%d" % q, [128, 8, TS], BF16) for q in range(2)]
        yaT = SB("yaT", [128, 8, TS], BF16)
        st = SB("st", [128, 4], F32)
        xlb = [SB("xlb%d" % f, [128, TS + 3], F32) for f in range(8)]
        gl8, xa8, r8, ig8, a8, a28, bb8, hsq8 = [SB(n_, [128, 4, TS], F32) for n_ in ("gl", "xa", "r", "ig", "a", "a2", "bb", "hsq")]
        xab8 = SB("xab", [128, 4, TS], BF16)
        psT = PS("psT", [128, 1024], BF16)
        pgs = [PS("pg%d" % q, [128, 512]) for q in range(2)]
        pxs = [PS("px%d" % q, [128, 512]) for q in range(2)]

        for dk in range(8):
            kb.dma("pool", wi[:, dk, :], c.d["even_w_in"][i, dk * 128:(dk + 1) * 128, 0:2048])
        kb.dma("pool", gaw[:], c.d["lru_gate_a_w"][i].rearrange("h i j -> i h j"))
        kb.dma("pool", gxw[:], c.d["lru_gate_x_w"][i].rearrange("h i j -> i h j"))
        kb.dma("sp", lcw[:], c.d["lcw"][i])
        kb.dma("sp", lcb[:], c.d["lcb"][i])
        kb.dma("sp", gab[:], c.d["gab"][i])
        kb.dma("sp", gxb[:], c.d["gxb"][i])
        kb.dma("sp", lam[:], c.d["lam"][i])
        kb.dma("sp", gbc[:], c.d["even_norm_g"][i, :].partition_broadcast(128))
        c.after_weights()
        kb.act(c1[:], lam[:], AF.Exp, scale=-1.0)
        kb.act(c1[:], c1[:], AF.Ln, bias=1.0)
        kb.ts("dve", c2[:], c1[:], -16.0, ALU.mult)
        kb.ts("dve", c1[:], c1[:], -8.0, ALU.mult)
        kb.op("dve", "memset", ap=hl[:], constant=0.0, wk=(hl.name,))
        for f in range(8):
            kb.op("dve", "memset", ap=xlb[f][:, 0:3], constant=0.0, wk=(xlb[f].name,))
        NSC = S // TS

        def front(sc):
            for tt in range(NT):
                r0 = sc * TS + tt * 128
                ht = ht2[tt % 2]
                kb.dma("sp", ht[:], src[r0:r0 + 128, :], rk=((src_key, r0 // 128),))
                rmsnorm_tile(c, ht[:], gbc[:], xn[:], st[:, 0:1], st[:, 1:2])
                transpose_to(c, xn, xnT2[sc % 2][:, :, tt * 128:(tt + 1) * 128], psT)

        front(0)
        for sc in range(NSC):
            xnT = xnT2[sc % 2]
            for g in range(2):
                for fl in range(4):
                    f = 4 * g + fl
                    xb_ = xlb[f]
                    pg_, px_ = pgs[f % 2], pxs[f % 2]
                    for dk in range(8):
                        kb.mm(pg_[:, 0:TS], wi[:, dk, f * 128:(f + 1) * 128], xnT[:, dk, :], dk == 0, dk == 7)
                    for dk in range(8):
                        kb.mm(px_[:, 0:TS], wi[:, dk, 1024 + f * 128:1024 + (f + 1) * 128], xnT[:, dk, :], dk == 0, dk == 7)
                    kb.act(gl8[:, fl, :], pg_[:, 0:TS], AF.Gelu_apprx_tanh)
                    kb.copy("act", xb_[:, 3:TS + 3], px_[:, 0:TS])
                    kb.ts("dve", xa8[:, fl, :], xb_[:, 3:TS + 3], lcw[:, f, 3:4], ALU.mult, lcb[:, f:f + 1], ALU.add)
                    for k in range(3):
                        kb.stt(xa8[:, fl, :], xb_[:, k:k + TS], lcw[:, f, k:k + 1], xa8[:, fl, :], ALU.mult, ALU.add)
                    kb.copy("act", xb_[:, 0:3], xb_[:, TS:TS + 3])
                    kb.copy("act", xab8[:, fl, :], xa8[:, fl, :])
                if g == 1 and sc + 1 < NSC:
                    front(sc + 1)
                for fl in range(4):
                    f = 4 * g + fl
                    pg_, px_ = pgs[f % 2], pxs[f % 2]
                    kb.mm(pg_[:, 0:TS], gaw[:, f, :], xab8[:, fl, :], True, True)
                    kb.mm(px_[:, 0:TS], gxw[:, f, :], xab8[:, fl, :], True, True)
                    kb.act(r8[:, fl, :], pg_[:, 0:TS], AF.Sigmoid, bias=gab[:, f:f + 1])
                    kb.act(ig8[:, fl, :], px_[:, 0:TS], AF.Sigmoid, bias=gxb[:, f:f + 1])
                for fl in range(4):
                    f = 4 * g + fl
                    kb.act(a8[:, fl, :], r8[:, fl, :], AF.Exp, scale=c1[:, f:f + 1])
                    kb.act(a28[:, fl, :], r8[:, fl, :], AF.Exp, scale=c2[:, f:f + 1])
                kb.ts("dve", a28[:], a28[:], 1.0, ALU.min, -1.0, ALU.mult)
                kb.act(a28[:], a28[:], AF.Sqrt, bias=1.0)
                kb.tt("dve", bb8[:], ig8[:], xa8[:], ALU.mult)
                kb.tt("dve", bb8[:], bb8[:], a28[:], ALU.mult)
                for fl in range(4):
                    f = 4 * g + fl
                    kb.op("dve", "tensor_tensor_scan", out=hsq8[:, fl, :], data0=a8[:, fl, :], data1=bb8[:, fl, :],
                          initial=hl[:, f:f + 1], op0=ALU.mult, op1=ALU.add)
                kb.copy("act", hl[:, 4 * g:4 * g + 4], hsq8[:, :, TS - 1])
                kb.tt("dve", yaT[:, 4 * g:4 * g + 4, :], gl8[:], hsq8[:], ALU.mult)
            for hb in range(TS // 256):
                blk = sc * (TS // 256) + hb
                kb.dma("sp", c.yaTd[blk], yaT[:, :, hb * 256:(hb + 1) * 256], wk=(("yaT", blk),))
        kb.barrier()


def phase_even_ssd(c, i, src, src_key):
    nc, kb, S = c.nc, c.kb, c.S
    TS = 256
    NT = TS // 128
    XO = 3072
    with ExitStack() as es:
        def SB(name, shape, dt):
            return es.enter_context(nc.sbuf_tensor(c.pfx + "s_" + name, shape, dt))

        def PS(name, shape, dt=F32):
            return es.enter_context(nc.psum_tensor(c.pfx + "s_" + name, shape, dt))
        wi = SB("wi", [128, 8, 2576], BF16)
        wo = SB("wo", [128, 16, 1024], BF16)
        scw = SB("scw", [128, 12, 4], F32)
        scb = SB("scb", [128, 12], F32)
        gbc = SB("g", [128, 1024], F32)
        ngbc = SB("ng", [128, 1024], F32)
        dtb = SB("dtb", [128, 16], F32)
        aneg = SB("aneg", [128, 16], F32)
        dsk = SB("dsk", [128, 16], F32)
        hres2 = [SB("hres%d" % q, [128, NT, 1024], F32) for q in range(2)]
        xn = SB("xn", [128, 1024], BF16)
        xnT2 = [SB("xnT%d" % q, [128, 8, TS], BF16) for q in range(2)]
        yabT2 = [SB("yabT%d" % q, [128, 16, TS], BF16) for q in range(2)]
        xsb = [SB("xsb%d" % f, [128, TS + 3], F32) for f in range(12)]
        cs2 = [SB("cs%d" % q, [128, TS], F32) for q in range(2)]
        xsT = SB("xsT", [128, 8, TS], F32)
        BT = SB("BT", [128, 2, TS], BF16)
        CT = SB("CT", [128, 2, TS], BF16)
        sz2 = [SB("sz%d" % q, [128, 1024], F32) for q in range(2)]
        xs_tok = SB("xs_tok", [128, 1024], F32)
        Btok = SB("Btok", [128, 256], BF16)
        xdt = SB("xdt", [128, 16, 64], BF16)
        xdtw = SB("xdtw", [128, 16, 64], BF16)
        mcb = SB("mcb", [128, 2, 128], F32)
        lseg4 = [SB("lseg%d" % q, [128, 4, 128], F32) for q in range(2)]
        E4 = [SB("E%d" % q, [128, 512], F32) for q in range(2)]
        MT4 = [SB("MT%d" % q, [128, 512], BF16) for q in range(2)]
        ysb = SB("ysb", [128, 1024], F32)
        tmp = SB("tmp", [128, 1024], F32)
        hs_f = SB("hs_f", [128, 1024], F32)
        hs_b = SB("hs_b", [128, 1024], BF16)
        yb = SB("yb", [128, 1024], BF16)
        hout2 = [SB("hout%d" % q, [128, 1024], F32) for q in range(2)]
        sm = SB("sm", [128, 8, 16], F32)
        ex = SB("ex", [128, 48], F32)
        st = SB("st", [128, 4], F32)
        psT = PS("psT", [128, 1024], BF16)
        pfa = PS("pfa", [128, 512])
        pfb = PS("pfb", [128, 512])
        pb1 = PS("pb1", [128, 1024])
        pb2 = PS("pb2", [128, 1024])
        psm = PS("psm", [128, 512])

        for dk in range(8):
            kb.dma("pool", wi[:, dk, :], c.d["even_w_in"][i, dk * 128:(dk + 1) * 128, 2048:4624])
        load_w_bf16(c, wo, c.d["even_w_out"][i], 16)
        kb.dma("sp", scw[:], c.d["scw"][i])
        kb.dma("sp", scb[:], c.d["scb"][i])
        kb.dma("sp", gbc[:], c.d["even_norm_g"][i, :].partition_broadcast(128))
        kb.dma("sp", ngbc[:], c.d["ssd_norm_g"][i, :].partition_broadcast(128))
        kb.dma("sp", dtb[:], c.d["ssd_dt_bias"][i, :].partition_broadcast(128))
        kb.dma("sp", aneg[:], c.d["ssd_a_log"][i, :].partition_broadcast(128))
        kb.dma("sp", dsk[:], c.d["ssd_d"][i, :].partition_broadcast(128))
        kb.act(aneg[:], aneg[:], AF.Exp)
        kb.ts("dve", aneg[:], aneg[:], -1.0, ALU.mult)
        kb.op("dve", "memset", ap=hs_f[:], constant=0.0, wk=(hs_f.name,))
        kb.op("dve", "memset", ap=hs_b[:], constant=0.0, wk=(hs_b.name,))
        for f in range(12):
            kb.op("dve", "memset", ap=xsb[f][:, 0:3], constant=0.0, wk=(xsb[f].name,))
        ZO, XB, DT0 = 0, 1024, 2560
        NSC = S // TS

        def front(sc):
            q = sc % 2
            kb.dma("sp", yabT2[q][:, 0:8, :], c.yaTd[sc], rk=(("yaT", sc),))
            for tt in range(NT):
                r0 = sc * TS + tt * 128
                kb.dma("sp", hres2[q][:, tt, :], src[r0:r0 + 128, :], rk=((src_key, r0 // 128),))
                rmsnorm_tile(c, hres2[q][:, tt, :], gbc[:], xn[:], st[:, 0:1], st[:, 1:2])
                transpose_to(c, xn, xnT2[q][:, :, tt * 128:(tt + 1) * 128], psT)

        def back(sc):
            q = sc % 2
            for tt in range(NT):
                r0 = sc * TS + tt * 128
                for half in range(2):
                    for kc in range(16):
                        kb.mm(pb2[:, half * 512:(half + 1) * 512], yabT2[q][:, kc, tt * 128:(tt + 1) * 128],
                              wo[:, kc, half * 512:(half + 1) * 512], kc == 0, kc == 15)
                kb.tt("dve", hout2[tt % 2][:], pb2[:], hres2[q][:, tt, :], ALU.add)
                kb.dma("sp", c.hbuf[r0:r0 + 128, :], hout2[tt % 2][:], wk=(("h", r0 // 128),))

        front(0)
        for sc in range(NSC):
            t0 = sc * TS
            hres, xnT, yabT = hres2[sc % 2], xnT2[sc % 2], yabT2[sc % 2]
            for f in range(12):
                pf = pfa if f % 2 == 0 else pfb
                xb_ = xsb[f]
                for dk in range(8):
                    kb.mm(pf[:, 0:TS], wi[:, dk, XB + f * 128:XB + (f + 1) * 128], xnT[:, dk, :], dk == 0, dk == 7)
                cs = cs2[f % 2]
                kb.copy("act", xb_[:, 3:TS + 3], pf[:, 0:TS])
                kb.ts("dve", cs[:], xb_[:, 3:TS + 3], scw[:, f, 3:4], ALU.mult, scb[:, f:f + 1], ALU.add)
                for k in range(3):
                    kb.stt(cs[:], xb_[:, k:k + TS], scw[:, f, k:k + 1], cs[:], ALU.mult, ALU.add)
                kb.copy("act", xb_[:, 0:3], xb_[:, TS:TS + 3])
                if f < 8:
                    dst = xsT[:, f, :]
                elif f < 10:
                    dst = BT[:, f - 8, :]
                else:
                    dst = CT[:, f - 10, :]
                kb.act(dst, cs[:], AF.Silu)
                if f == 3 and sc > 0:
                    back(sc - 1)
                if f == 8 and sc + 1 < NSC:
                    front(sc + 1)
            for tt in range(NT):
                cols = slice(tt * 128, (tt + 1) * 128)
                for half in range(2):
                    for dk in range(8):
                        kb.mm(pb1[:, half * 512:(half + 1) * 512], xnT[:, dk, cols],
                              wi[:, dk, ZO + half * 512:ZO + (half + 1) * 512], dk == 0, dk == 7)
                kb.act(sz2[tt][:], pb1[:], AF.Silu)
            for tt in range(NT):
                cols = slice(tt * 128, (tt + 1) * 128)
                sz = sz2[tt]
                pdt = psm[:, 432:448]
                for dk in range(8):
                    kb.mm(pdt, xnT[:, dk, cols], wi[:, dk, DT0:DT0 + 16], dk == 0, dk == 7)
                t1, ax, dt_, dA, dtw = sm[:, 0, :], sm[:, 1, :], sm[:, 2, :], sm[:, 3, :], sm[:, 4, :]
                kb.tt("dve", t1, pdt, dtb[:], ALU.add)
                kb.ts("dve", ax, t1, -1.0, ALU.mult)
                kb.tt("dve", ax, ax, t1, ALU.max)
                kb.act(ax, ax, AF.Exp, scale=-1.0)
                kb.act(ax, ax, AF.Ln, bias=1.0)
                kb.stt(dt_, t1, 0.0, ax, ALU.max, ALU.add)
                kb.tt("dve", dA, dt_, aneg[:], ALU.mult)
                for f in range(8):
                    kb.tr(pb2[:, f * 128:(f + 1) * 128], xsT[:, f, cols], c.ident_f[:])
                for g in range(2):
                    kb.tr(psT[:, g * 128:(g + 1) * 128], BT[:, g, cols], c.ident_b[:])
                for g in range(2):
                    kb.mm(psm[:, g * 128:(g + 1) * 128], BT[:, g, cols], CT[:, g, cols], True, True)
                kb.mm(psm[:, 384:400], c.triU[:], dA, True, True)
                kb.mm(psm[:, 400:416], c.maskgt[:], dA, True, True)
                kb.mm(psm[:, 416:432], c.ones_f[:], dA, True, True)
                kb.copy("act", xs_tok[:], pb2[:])
                kb.copy("act", Btok[:], psT[:, 0:256])
                kb.act(ex[:], psm[:, 384:432], AF.Exp)
                expcum, dte, cd = ex[:, 0:16], ex[:, 16:32], ex[:, 32:48]
                kb.tt("dve", dtw, dt_, dte, ALU.mult)
                kb.tt("dve", mcb[:], psm[:, 0:256].rearrange("p (g l) -> p g l", g=2),
                      bc(c.triU[:].unsqueeze(1), [128, 2, 128]), ALU.mult)
                xs3 = xs_tok[:].rearrange("p (j q) -> p j q", j=16)
                kb.tt("dve", xdt[:], xs3, bc(dt_.unsqueeze(2), [128, 16, 64]), ALU.mult)
                kb.tt("dve", xdtw[:], xs3, bc(dtw.unsqueeze(2), [128, 16, 64]), ALU.mult)
                def hg_dve(q):
                    j0 = q * 4
                    kb.tt("dve", lseg4[q % 2][:], bc(c.maskgt[:].unsqueeze(1), [128, 4, 128]),
                          bc(dA[:, j0:j0 + 4].unsqueeze(2), [128, 4, 128]), ALU.mult)

                def hg_seg(q):
                    pf = (pfa, pfb)[q % 2]
                    for u in range(4):
                        kb.mm(pf[:, u * 128:(u + 1) * 128], lseg4[q % 2][:, u, :], c.triU[:], True, True)

                def hg_exp(q):
                    kb.act(E4[q % 2][:], (pfa, pfb)[q % 2][:], AF.Exp)

                def hg_mt(q):
                    g = (q * 4) // 8
                    kb.tt("pool", MT4[q % 2][:].rearrange("p (u l) -> p u l", u=4),
                          E4[q % 2][:].rearrange("p (u l) -> p u l", u=4),
                          bc(mcb[:, g, :].unsqueeze(1), [128, 4, 128]), ALU.mult)

                def hg_y(q):
                    for u in range(4):
                        j = q * 4 + u
                        kb.mm(pb1[:, j * 64:(j + 1) * 64], MT4[q % 2][:, u * 128:(u + 1) * 128], xdt[:, j, :], True, True)
                for pr_ in range(2):
                    qa, qb = 2 * pr_, 2 * pr_ + 1
                    for fn in (hg_dve, hg_seg, hg_exp, hg_mt, hg_y):
                        fn(qa)
                        fn(qb)
                for g in range(2):
                    kb.mm(pb2[:, g * 512:(g + 1) * 512], CT[:, g, cols], hs_b[:, g * 512:(g + 1) * 512], True, True)
                kb.tt("dve", tmp[:].rearrange("p (j q) -> p j q", j=16), pb2[:].rearrange("p (j q) -> p j q", j=16),
                      bc(expcum.unsqueeze(2), [128, 16, 64]), ALU.mult)
                kb.tt("dve", ysb[:], pb1[:], tmp[:], ALU.add)
                kb.tt("dve", tmp[:].rearrange("p (j q) -> p j q", j=16), xs3, bc(dsk[:].unsqueeze(2), [128, 16, 64]), ALU.mult)
                kb.tt("dve", ysb[:], ysb[:], tmp[:], ALU.add)
                for g in range(2):
                    kb.mm(pb2[:, g * 512:(g + 1) * 512], Btok[:, g * 128:(g + 1) * 128],
                          xdtw[:, g * 8:(g + 1) * 8, :].rearrange("p j q -> p (j q)"), True, True)
                hs3 = hs_f[:].rearrange("p (j q) -> p j q", j=16)
                kb.tt("dve", hs3, hs3, bc(cd.unsqueeze(2), [128, 16, 64]), ALU.mult)
                kb.tt("dve", hs_f[:], hs_f[:], pb2[:], ALU.add)
                kb.copy("act", hs_b[:], hs_f[:])
                kb.tt("dve", tmp[:], ysb[:], sz[:], ALU.mult)
                rmsnorm_tile(c, tmp[:], ngbc[:], yb[:], st[:, 2:3], st[:, 3:4])
                for k in range(8):
                    kb.tr(psT[:, k * 128:(k + 1) * 128], yb[:, k * 128:(k + 1) * 128], c.ident_b[:])
                kb.copy("act", yabT[:, 8:16, cols], psT[:].rearrange("p (k t) -> p k t", k=8))
        back(NSC - 1)
        kb.barrier()


def _fm(a, k_last=True):
    a = np.asarray(a)
    if a.ndim == 2:
        E, C = a.shape
        return np.ascontiguousarray(a.reshape(E, C // 128, 128).transpose(0, 2, 1))
    E, K, C = a.shape
    return np.ascontiguousarray(a.reshape(E, K, C // 128, 128).transpose(0, 3, 2, 1))


def prep_shared(inp, layers=(0, 1, 2, 3)):
    layers = list(layers)
    sh = {}
    for k in ("even_norm_g", "even_w_in", "lru_gate_a_w", "lru_gate_x_w", "ssd_dt_bias", "ssd_a_log", "ssd_d",
              "ssd_norm_g", "even_w_out", "odd_norm_g", "odd_w_in", "odd_w_out"):
        sh[k] = np.ascontiguousarray(inp[k])
    sh["lcw"] = _fm(inp["lru_conv_w"])
    sh["lcb"] = _fm(inp["lru_conv_b"])
    sh["gab"] = _fm(inp["lru_gate_a_b"])
    sh["gxb"] = _fm(inp["lru_gate_x_b"])
    sh["lam"] = _fm(inp["lru_lambda"])
    sh["scw"] = _fm(inp["ssd_conv_w"])
    sh["scb"] = _fm(inp["ssd_conv_b"])
    sh["ocw"] = _fm(inp["odd_conv_w"])
    sh["ffn_norm_g"] = np.ascontiguousarray(np.asarray(inp["ffn_norm_g"])[layers])
    sh["peer_w_query"] = np.ascontiguousarray(np.asarray(inp["peer_w_query"])[layers])
    sh["peer_sub_keys"] = np.ascontiguousarray(np.asarray(inp["peer_sub_keys"])[layers])
    u = np.asarray(inp["peer_u"])[layers]
    L = u.shape[0]
    sh["peer_ut"] = np.ascontiguousarray(u.reshape(L, 128, 128, 8, 128).transpose(0, 1, 4, 3, 2)).reshape(L, 128, 128, 1024)
    sh["peer_v"] = np.ascontiguousarray(np.asarray(inp["peer_v"])[layers])
    sh["final_norm_g"] = np.ascontiguousarray(np.asarray(inp["final_norm_g"]).reshape(1, 1024))
    return sh


FULL_PLAN = [("even", 0), ("peer", 0), ("odd", 0), ("peer", 1), ("even", 1), ("peer", 2), ("odd", 1), ("peer", 3, "final")]


def kernel(**inputs):
    x = np.asarray(inputs["x"])
    B, S, _ = x.shape
    sh = prep_shared(inputs)
    nc = build_program(S, FULL_PLAN)
    in_maps = []
    for b in range(B):
        m = dict(sh)
        m["x"] = np.ascontiguousarray(x[b])
        in_maps.append(m)
    res = run_bass_kernel_spmd(nc, in_maps, core_ids=list(range(B)))
    return np.stack([np.asarray(r["y"]) for r in res.results], axis=0).astype(np.float32)
```

```python
from contextlib import ExitStack
import numpy as np
import concourse.bass as bass
import concourse.mybir as mybir
from concourse.bass_utils import run_bass_kernel_spmd

F32 = mybir.dt.float32
BF16 = mybir.dt.bfloat16
I32 = mybir.dt.int32
U32 = mybir.dt.uint32
U8 = mybir.dt.uint8
AF = mybir.ActivationFunctionType
ALU = mybir.AluOpType
AX = mybir.AxisListType

D = 1024
NCORES = 8
EVEN_IN = 4624
NEG = -1.0e30
WRITE_KW = ("out", "accum_out", "out_max", "out_indices")


def _space(ap):
    n = type(ap.tensor).__name__
    if "SB" in n:
        return "sb"
    if "PSum" in n:
        return "ps"
    return "dram"


class KB:
    NDS = 48
    SAME_RAW = True

    def __init__(self, nc):
        self.nc = nc
        self.eng = dict(pe=nc.tensor, dve=nc.vector, act=nc.scalar, pool=nc.gpsimd, sp=nc.sync)
        self.esem = {k: nc.alloc_semaphore("es_" + k) for k in self.eng}
        self.ecnt = {k: 0 for k in self.eng}
        self.seen = {k: {} for k in self.eng}
        self.dsems = [nc.alloc_semaphore("ds%d" % i) for i in range(self.NDS)]
        self.dval = [0] * self.NDS
        self.dpool = {"sp": list(range(0, 32)), "pool": list(range(32, 44)), "act": list(range(44, 48))}
        self.dnext = {k: 0 for k in self.dpool}
        self.lastw = {}
        self.rd_eng = {}
        self.rd_dma = {}
        self.nins = 0

    def _wait(self, e, tok):
        sem, val, src, sid = tok
        if self.seen[e].get(sid, 0) >= val:
            return
        self.eng[e].wait_ge(sem, val)
        self.seen[e][sid] = val
        self.nins += 1

    def _sync(self, e, reads, writes):
        for k in reads:
            t = self.lastw.get(k)
            if t is not None and not (t[2] == e and (e == "pe" or not self.SAME_RAW)):
                self._wait(e, t)
        for k in writes:
            t = self.lastw.get(k)
            if t is not None and t[2] != e:
                self._wait(e, t)
            for src, t in self.rd_eng.get(k, {}).items():
                if src != e:
                    self._wait(e, t)
            for t in self.rd_dma.get(k, ()):
                self._wait(e, t)

    def _record(self, tok, reads, writes):
        for k in writes:
            self.lastw[k] = tok
            self.rd_eng[k] = {}
            self.rd_dma[k] = []
        for k in reads:
            if k in writes:
                continue
            if tok[2] is None:
                self.rd_dma.setdefault(k, []).append(tok)
            else:
                self.rd_eng.setdefault(k, {})[tok[2]] = tok

    def op(self, e, method, rk=(), wk=(), xr=None, xw=None, **kw):
        reads, writes = set(rk), set(wk)
        for k, v in kw.items():
            if isinstance(v, bass.AP):
                (writes if k in WRITE_KW else reads).add(v.tensor.name)
        if xr is not None:
            reads = set(xr)
        if xw is not None:
            writes = set(xw)
        self._sync(e, reads, writes)
        ins = getattr(self.eng[e], method)(**kw)
        self.ecnt[e] += 1
        ins.then_inc(self.esem[e], 1)
        self.nins += 1
        tok = (self.esem[e], self.ecnt[e], e, "e_" + e)
        self._record(tok, reads, writes)
        return ins

    def dma(self, q, out, in_, rk=(), wk=()):
        reads, writes = set(rk), set(wk)
        if _space(out) != "dram":
            writes.add(out.tensor.name)
        if _space(in_) != "dram":
            reads.add(in_.tensor.name)
        self._sync(q, reads, writes)
        pl = self.dpool[q]
        i = pl[self.dnext[q] % len(pl)]
        self.dnext[q] += 1
        sid = "d_%d" % i
        if self.dval[i] > 0:
            self._wait(q, (self.dsems[i], self.dval[i], None, sid))
        self.dval[i] += 16
        self.eng[q].dma_start(out=out, in_=in_).then_inc(self.dsems[i], 16)
        self.nins += 1
        tok = (self.dsems[i], self.dval[i], None, sid)
        self._record(tok, reads, writes)

    def barrier(self):
        for e in self.eng:
            for e2 in self.eng:
                if e2 != e and self.ecnt[e2] > 0:
                    self._wait(e, (self.esem[e2], self.ecnt[e2], e2, "e_" + e2))
            for i in range(self.NDS):
                if self.dval[i] > 0:
                    self._wait(e, (self.dsems[i], self.dval[i], None, "d_%d" % i))
        self.lastw.clear()
        self.rd_eng.clear()
        self.rd_dma.clear()

    def mm(self, out, lhsT, rhs, start, stop):
        return self.op("pe", "matmul", out=out, lhsT=lhsT, rhs=rhs, start=start, stop=stop)

    def tr(self, out, in_, ident):
        return self.op("pe", "transpose", out=out, in_=in_, identity=ident)

    def act(self, out, in_, func, **kw):
        return self.op("act", "activation", out=out, in_=in_, func=func, **kw)

    def tt(self, e, out, in0, in1, op):
        return self.op(e, "tensor_tensor", out=out, in0=in0, in1=in1, op=op)

    def ts(self, e, out, in0, s1, op0, s2=None, op1=None, **kw):
        if op1 is None:
            return self.op(e, "tensor_scalar", out=out, in0=in0, scalar1=s1, scalar2=None, op0=op0, **kw)
        return self.op(e, "tensor_scalar", out=out, in0=in0, scalar1=s1, scalar2=s2, op0=op0, op1=op1, **kw)

    def stt(self, out, in0, scalar, in1, op0, op1):
        return self.op("dve", "scalar_tensor_tensor", out=out, in0=in0, scalar=scalar, in1=in1, op0=op0, op1=op1)

    def copy(self, e, out, in_):
        if e == "act":
            return self.op("act", "activation", out=out, in_=in_, func=AF.Copy)
        return self.op(e, "tensor_copy", out=out, in_=in_)


class Ctx:
    pass


def bc(ap, shape):
    return ap.to_broadcast(list(shape))


def setup_consts(c):
    nc, kb = c.nc, c.kb
    A = nc.alloc_sbuf_tensor
    c.iota_i = A("c_iota_i", [128, 128], I32)
    c.part_i = A("c_part_i", [128, 1], I32)
    c.iota_f = A("c_iota_f", [128, 128], F32)
    c.iota_b = A("c_iota_b", [128, 128], BF16)
    c.part_f = A("c_part_f", [128, 1], F32)
    c.ident_f = A("c_ident_f", [128, 128], F32)
    c.ident_b = A("c_ident_b", [128, 128], BF16)
    c.triU = A("c_triU", [128, 128], F32)
    c.maskgt = A("c_maskgt", [128, 128], F32)
    c.ones_f = A("c_ones_f", [128, 128], F32)
    c.junk_b = A("c_junk_b", [128, 1024], BF16)
    c.eps_t = A("c_eps_t", [128, 1], F32)
    kb.op("dve", "memset", ap=c.eps_t[:], constant=1e-6, wk=("c_eps_t",))
    kb.op("pool", "iota", out=c.iota_i[:], pattern=[[1, 128]], base=0, channel_multiplier=0)
    kb.op("pool", "iota", out=c.part_i[:], pattern=[[0, 1]], base=0, channel_multiplier=1)
    kb.copy("dve", c.iota_f[:], c.iota_i[:])
    kb.copy("dve", c.iota_b[:], c.iota_i[:])
    kb.copy("dve", c.part_f[:], c.part_i[:])
    kb.ts("dve", c.ident_f[:], c.iota_f[:], c.part_f[:, 0:1], ALU.is_equal)
    kb.copy("dve", c.ident_b[:], c.ident_f[:])
    kb.ts("dve", c.triU[:], c.iota_f[:], c.part_f[:, 0:1], ALU.is_ge)
    kb.ts("dve", c.maskgt[:], c.iota_f[:], c.part_f[:, 0:1], ALU.is_lt)
    kb.op("dve", "memset", ap=c.ones_f[:], constant=1.0, wk=("c_ones_f",))


def rmsnorm_tile(c, ht, gbc, xn, ssq, rstd):
    kb = c.kb
    kb.act(c.junk_b[:], ht, AF.Square, accum_out=ssq)
    kb.act(rstd, ssq, AF.Ln, scale=1.0 / D, bias=c.eps_t[:, 0:1])
    kb.act(rstd, rstd, AF.Exp, scale=-0.5)
    kb.stt(xn, ht, rstd, gbc, ALU.mult, ALU.mult)


def transpose_to(c, src_b, dstT, psT, n=8):
    kb = c.kb
    for k in range(n):
        kb.tr(psT[:, k * 128:(k + 1) * 128], src_b[:, k * 128:(k + 1) * 128], c.ident_b[:])
    kb.copy("act", dstT, psT[:, 0:n * 128].rearrange("p (k t) -> p k t", k=n))


def load_w_bf16(c, dst, src2d, ndk):
    for dk in range(ndk):
        c.kb.dma("pool", dst[:, dk, :], src2d[dk * 128:(dk + 1) * 128, :])


def phase_odd(c, i, src, src_key):
    nc, kb, S = c.nc, c.kb, c.S
    TS = min(512, S)
    NT = TS // 128
    NSC = S // TS
    with ExitStack() as es:
        def SB(name, shape, dt):
            return es.enter_context(nc.sbuf_tensor(c.pfx + "o_" + name, shape, dt))

        def PS(name, shape, dt=F32):
            return es.enter_context(nc.psum_tensor(c.pfx + "o_" + name, shape, dt))
        wi = SB("wi", [128, 8, 3072], BF16)
        wo = SB("wo", [128, 8, 1024], BF16)
        cw = SB("cw", [128, 8, 3], F32)
        gbc = SB("g", [128, 1024], F32)
        hres = [SB("h%d" % q, [128, NT, 1024], F32) for q in range(2)]
        xnT = [SB("xnT%d" % q, [128, 8, TS], BF16) for q in range(2)]
        uT = [SB("uT%d" % q, [128, 8, TS], BF16) for q in range(2)]
        xn = [SB("xn%d" % q, [128, 1024], BF16) for q in range(2)]
        csb = [SB("csb%d" % q, [128, TS], F32) for q in range(2)]
        acc = [SB("acc%d" % q, [128, TS], F32) for q in range(2)]
        hout = [SB("ho%d" % q, [128, 1024], F32) for q in range(2)]
        st = [SB("st%d" % q, [128, 4], F32) for q in range(2)]
        cvb = [SB("cvb%d" % f, [128, TS + 2], F32) for f in range(8)]
        po = PS("po", [128, 1024])
        psT = po[:, 0:512].bitcast(BF16)
        pb = [PS("pb%d" % q, [128, 512]) for q in range(2)]
        pc = [PS("pc%d" % q, [128, 512]) for q in range(2)]
        pv = [PS("pv%d" % q, [128, 512]) for q in range(2)]
        load_w_bf16(c, wi, c.d["odd_w_in"][i], 8)
        load_w_bf16(c, wo, c.d["odd_w_out"][i], 8)
        kb.dma("sp", cw[:], c.d["ocw"][i])
        kb.dma("sp", gbc[:], c.d["odd_norm_g"][i, :].partition_broadcast(128))
        c.after_weights()
        for f in range(8):
            kb.op("dve", "memset", ap=cvb[f][:, 0:2], constant=0.0, wk=(cvb[f].name,))

        def front_tile(sc, tt):
            q = sc % 2
            r0 = sc * TS + tt * 128
            kb.dma("sp", hres[q][:, tt, :], src[r0:r0 + 128, :], rk=((src_key, r0 // 128),))
            rmsnorm_tile(c, hres[q][:, tt, :], gbc[:], xn[tt % 2][:], st[tt % 2][:, 0:1], st[tt % 2][:, 1:2])
            for k in range(8):
                kb.tr(psT[:, k * 128:(k + 1) * 128], xn[tt % 2][:, k * 128:(k + 1) * 128], c.ident_b[:])
            kb.copy("act", xnT[q][:, :, tt * 128:(tt + 1) * 128], psT.rearrange("p (k t) -> p k t", k=8))

        def mid(sc, f):
            q = sc % 2
            p_ = f % 2
            for (pp, off) in ((pb[p_], 0), (pc[p_], 1024), (pv[p_], 2048)):
                for dk in range(8):
                    kb.mm(pp[:, 0:TS], wi[:, dk, off + f * 128: off + (f + 1) * 128], xnT[q][:, dk, :], dk == 0, dk == 7)
            kb.copy("act", csb[p_][:], pc[p_][:, 0:TS])
            kb.tt("dve", cvb[f][:, 2:TS + 2], pv[p_][:, 0:TS], csb[p_][:], ALU.mult)
            kb.ts("dve", acc[p_][:], cvb[f][:, 0:TS], cw[:, f, 0:1], ALU.mult)
            kb.stt(acc[p_][:], cvb[f][:, 1:TS + 1], cw[:, f, 1:2], acc[p_][:], ALU.mult, ALU.add)
            kb.stt(acc[p_][:], cvb[f][:, 2:TS + 2], cw[:, f, 2:3], acc[p_][:], ALU.mult, ALU.add)
            kb.tt("dve", uT[q][:, f, :], pb[p_][:, 0:TS], acc[p_][:], ALU.mult)
            kb.copy("act", cvb[f][:, 0:2], cvb[f][:, TS:TS + 2])

        def back(sc):
            q = sc % 2
            for tt in range(NT):
                r0 = sc * TS + tt * 128
                for half in range(2):
                    for kc in range(8):
                        kb.mm(po[:, half * 512:(half + 1) * 512], uT[q][:, kc, tt * 128:(tt + 1) * 128],
                              wo[:, kc, half * 512:(half + 1) * 512], kc == 0, kc == 7)
                kb.tt("dve", hout[tt % 2][:], po[:], hres[q][:, tt, :], ALU.add)
                kb.dma("sp", c.hbuf[r0:r0 + 128, :], hout[tt % 2][:], wk=(("h", r0 // 128),))

        for tt in range(NT):
            front_tile(0, tt)
        for sc in range(NSC):
            for f in range(8):
                mid(sc, f)
                if f == 3:
                    if sc > 0:
                        back(sc - 1)
                if f >= 4 and sc + 1 < NSC and (f - 4) < NT:
                    front_tile(sc + 1, f - 4)
        back(NSC - 1)
        kb.barrier()


def phase_peer_route(c, l, rt):
    nc, kb, S = c.nc, c.kb, c.S
    iT, jT, gT = rt
    with ExitStack() as es:
        wq = es.enter_context(nc.sbuf_tensor(c.pfx + "r_wq", [128, 8, 2048], BF16))
        kn = es.enter_context(nc.sbuf_tensor(c.pfx + "r_kn", [128, 16, 128], F32))
        kT = es.enter_context(nc.sbuf_tensor(c.pfx + "r_kT", [128, 16, 128], BF16))
        gbc = es.enter_context(nc.sbuf_tensor(c.pfx + "r_g", [128, 1024], F32))
        ht_2 = [es.enter_context(nc.sbuf_tensor(c.pfx + "r_h%d" % q, [128, 1024], F32)) for q in range(2)]
        xn_2 = [es.enter_context(nc.sbuf_tensor(c.pfx + "r_xn%d" % q, [128, 1024], BF16)) for q in range(2)]
        xnT_2 = [es.enter_context(nc.sbuf_tensor(c.pfx + "r_xnT%d" % q, [128, 8, 128], BF16)) for q in range(2)]
        qT_2 = [es.enter_context(nc.sbuf_tensor(c.pfx + "r_qT%d" % q, [128, 16, 128], BF16)) for q in range(2)]
        sc_2 = [es.enter_context(nc.sbuf_tensor(c.pfx + "r_sc%d" % q, [128, 2048], F32)) for q in range(2)]
        wk_ = es.enter_context(nc.sbuf_tensor(c.pfx + "r_wk", [128, 2048], F32))
        sv = es.enter_context(nc.sbuf_tensor(c.pfx + "r_sv", [128, 16, 16], F32))
        si = es.enter_context(nc.sbuf_tensor(c.pfx + "r_si", [128, 16, 16], U32))
        sif = es.enter_context(nc.sbuf_tensor(c.pfx + "r_sif", [128, 16, 16], F32))
        cand = es.enter_context(nc.sbuf_tensor(c.pfx + "r_cand", [128, 8, 112], F32))
        cwk = es.enter_context(nc.sbuf_tensor(c.pfx + "r_cwk", [128, 8, 112], F32))
        hif, lof, mB, dd, ee = [es.enter_context(nc.sbuf_tensor(c.pfx + "r_" + n_, [128, 8, 16], F32))
                                for n_ in ("hif", "lof", "mB", "dd", "ee")]
        tv = es.enter_context(nc.sbuf_tensor(c.pfx + "r_tv", [128, 8, 16], F32))
        tp = es.enter_context(nc.sbuf_tensor(c.pfx + "r_tp", [128, 8, 16], U32))
        k1u = es.enter_context(nc.sbuf_tensor(c.pfx + "r_k1", [128, 8, 16], U32))
        k2u = es.enter_context(nc.sbuf_tensor(c.pfx + "r_k2", [128, 8, 16], U32))
        k1f = es.enter_context(nc.sbuf_tensor(c.pfx + "r_k1f", [128, 8, 16], F32))
        k2f = es.enter_context(nc.sbuf_tensor(c.pfx + "r_k2f", [128, 8, 16], F32))
        oh = es.enter_context(nc.sbuf_tensor(c.pfx + "r_oh", [128, 8, 16, 16], BF16))
        oh2 = es.enter_context(nc.sbuf_tensor(c.pfx + "r_oh2", [128, 8, 16, 16], BF16))
        i_f = es.enter_context(nc.sbuf_tensor(c.pfx + "r_if", [128, 128], F32))
        j_f = es.enter_context(nc.sbuf_tensor(c.pfx + "r_jf", [128, 128], F32))
        g_f = es.enter_context(nc.sbuf_tensor(c.pfx + "r_gf", [128, 128], F32))
        sm = es.enter_context(nc.sbuf_tensor(c.pfx + "r_sm", [128, 8, 4], F32))
        st2 = [es.enter_context(nc.sbuf_tensor(c.pfx + "r_st%d" % q, [128, 4], F32)) for q in range(2)]
        psT = es.enter_context(nc.psum_tensor(c.pfx + "r_psT", [128, 1024], BF16))
        pq = es.enter_context(nc.psum_tensor(c.pfx + "r_pq", [128, 2048], F32))
        ptr = es.enter_context(nc.psum_tensor(c.pfx + "r_ptr", [128, 512], F32))
        load_w_bf16(c, wq, c.d["peer_w_query"][l], 8)
        c.after_weights()
        kb.dma("sp", kn[:], c.d["peer_sub_keys"][l].rearrange("h p n d -> n (h p) d"))
        kb.dma("sp", gbc[:], c.d["ffn_norm_g"][l, :].partition_broadcast(128))
        for hp in range(16):
            kb.tr(ptr[:, (hp % 4) * 128:(hp % 4 + 1) * 128], kn[:, hp, :], c.ident_f[:])
            if hp % 4 == 3:
                kb.copy("act", kT[:, hp - 3:hp + 1, :], ptr[:].rearrange("p (k t) -> p k t", k=4))
        sv4 = sv[:].rearrange("p (h two) k -> p h two k", two=2)
        sif4 = sif[:].rearrange("p (h two) k -> p h two k", two=2)
        iota16 = c.iota_f[:, 0:16]
        def front(tt):
            r0 = tt * 128
            ht, xn, xnT, qT, sc = ht_2[tt % 2], xn_2[tt % 2], xnT_2[tt % 2], qT_2[tt % 2], sc_2[tt % 2]
            kb.dma("sp", ht[:], c.hbuf[r0:r0 + 128, :], rk=(("h", tt),))
            rmsnorm_tile(c, ht[:], gbc[:], xn[:], st2[tt % 2][:, 0:1], st2[tt % 2][:, 1:2])
            transpose_to(c, xn, xnT[:], psT)
            blk, off = tt // 2, (tt % 2) * 128
            kb.dma("sp", c.xnTd[blk, :, :, off:off + 128], xnT[:], wk=(("xnT", blk, tt % 2),))
            for hp in range(16):
                for dk in range(8):
                    kb.mm(pq[:, hp * 128:(hp + 1) * 128], wq[:, dk, hp * 128:(hp + 1) * 128], xnT[:, dk, :], dk == 0, dk == 7)
            kb.copy("act", qT[:], pq[:].rearrange("p (k t) -> p k t", k=16))
            for hp in range(16):
                kb.mm(pq[:, hp * 128:(hp + 1) * 128], qT[:, hp, :], kT[:, hp, :], True, True)
            kb.copy("act", sc[:], pq[:])

        def back(tt):
            r0 = tt * 128
            sc = sc_2[tt % 2]
            SC = sc.name
            S_ = lambda hp: sc[:, hp * 128:(hp + 1) * 128]
            W_ = lambda hp: wk_[:, hp * 128:(hp + 1) * 128]
            for hp in range(16):
                kb.op("dve", "max", out=sv[:, hp, 0:8], in_=S_(hp), xr={SC}, xw={("sv", hp, 0)})
            for hp in range(16):
                kb.op("dve", "max_index", out=si[:, hp, 0:8], in_max=sv[:, hp, 0:8], in_values=S_(hp),
                      xr={SC, ("sv", hp, 0)}, xw={("si", hp, 0)})
            for hp in range(16):
                kb.op("dve", "match_replace", out=W_(hp), in_to_replace=sv[:, hp, 0:8], in_values=S_(hp), imm_value=NEG,
                      xr={SC, ("sv", hp, 0)}, xw={("wk", hp)})
            for hp in range(16):
                kb.op("dve", "max", out=sv[:, hp, 8:16], in_=W_(hp), xr={("wk", hp)}, xw={("sv", hp, 1)})
            for hp in range(16):
                kb.op("dve", "max_index", out=si[:, hp, 8:16], in_max=sv[:, hp, 8:16], in_values=W_(hp),
                      xr={("wk", hp), ("sv", hp, 1)}, xw={("si", hp, 1)})
            SV_ALL = {("sv", hp, q) for hp in range(16) for q in range(2)}
            SI_ALL = {("si", hp, q) for hp in range(16) for q in range(2)}
            kb.op("dve", "tensor_copy", out=sif[:], in_=si[:], xr=SI_ALL, xw={sif.name})
            candA = cand[:, :, 0:64].rearrange("p h (a b) -> p h a b", a=16)
            candB = cand[:, :, 64:112].rearrange("p h (a b) -> p h a b", a=12)
            kb.op("dve", "tensor_tensor", out=candA, in0=bc(sv4[:, :, 0, :].unsqueeze(3), [128, 8, 16, 4]),
                  in1=bc(sv4[:, :, 1, 0:4].unsqueeze(2), [128, 8, 16, 4]), op=ALU.add, xr=SV_ALL, xw={cand.name})
            kb.op("dve", "tensor_tensor", out=candB, in0=bc(sv4[:, :, 1, 4:16].unsqueeze(3), [128, 8, 12, 4]),
                  in1=bc(sv4[:, :, 0, 0:4].unsqueeze(2), [128, 8, 12, 4]), op=ALU.add, xr=SV_ALL, xw={cand.name})
            CD = cand.name
            for h in range(8):
                kb.op("dve", "max", out=tv[:, h, 0:8], in_=cand[:, h, :], xr={CD}, xw={("tv", h, 0)})
            for h in range(8):
                kb.op("dve", "max_index", out=tp[:, h, 0:8], in_max=tv[:, h, 0:8], in_values=cand[:, h, :],
                      xr={CD, ("tv", h, 0)}, xw={("tp", h, 0)})
            for h in range(8):
                kb.op("dve", "match_replace", out=cwk[:, h, :], in_to_replace=tv[:, h, 0:8], in_values=cand[:, h, :],
                      imm_value=NEG, xr={CD, ("tv", h, 0)}, xw={("cwk", h)})
            for h in range(8):
                kb.op("dve", "max", out=tv[:, h, 8:16], in_=cwk[:, h, :], xr={("cwk", h)}, xw={("tv", h, 1)})
            for h in range(8):
                kb.op("dve", "max_index", out=tp[:, h, 8:16], in_max=tv[:, h, 8:16], in_values=cwk[:, h, :],
                      xr={("cwk", h), ("tv", h, 1)}, xw={("tp", h, 1)})
            TV_ALL = {("tv", h, q) for h in range(8) for q in range(2)}
            TP_ALL = {("tp", h, q) for h in range(8) for q in range(2)}
            kb.op("dve", "tensor_single_scalar", out=k1u[:], in_=tp[:], scalar=2, op=ALU.logical_shift_right,
                  xr=TP_ALL, xw={k1u.name})
            kb.op("dve", "tensor_single_scalar", out=k2u[:], in_=tp[:], scalar=3, op=ALU.bitwise_and,
                  xr=TP_ALL, xw={k2u.name})
            i3 = i_f[:].rearrange("p (h k) -> p h k", h=8)
            j3 = j_f[:].rearrange("p (h k) -> p h k", h=8)
            g3 = g_f[:].rearrange("p (h k) -> p h k", h=8)
            kb.op("dve", "tensor_tensor", out=g3, in0=tv[:], in1=bc(tv[:, :, 0:1], [128, 8, 16]), op=ALU.subtract,
                  xr=TV_ALL, xw={g_f.name})
            kb.copy("dve", hif[:], k1u[:])
            kb.copy("dve", lof[:], k2u[:])
            kb.act(g3, g3, AF.Exp)
            kb.ts("dve", mB[:], hif[:], 16.0, ALU.is_ge)
            kb.tt("dve", dd[:], lof[:], hif[:], ALU.subtract)
            kb.ts("dve", ee[:], dd[:], -1.0, ALU.mult, -12.0, ALU.add)
            kb.tt("dve", dd[:], dd[:], mB[:], ALU.mult)
            kb.tt("dve", ee[:], ee[:], mB[:], ALU.mult)
            kb.tt("dve", k1f[:], dd[:], hif[:], ALU.add)
            kb.tt("dve", k2f[:], ee[:], lof[:], ALU.add)
            io4 = bc(iota16.unsqueeze(1).unsqueeze(1), [128, 8, 16, 16])
            kb.tt("dve", oh[:], io4, bc(k1f[:].unsqueeze(3), [128, 8, 16, 16]), ALU.is_equal)
            kb.tt("dve", oh2[:], io4, bc(k2f[:].unsqueeze(3), [128, 8, 16, 16]), ALU.is_equal)
            kb.tt("dve", oh[:], oh[:], bc(sif4[:, :, 0, :].unsqueeze(2), [128, 8, 16, 16]), ALU.mult)
            kb.tt("dve", oh2[:], oh2[:], bc(sif4[:, :, 1, :].unsqueeze(2), [128, 8, 16, 16]), ALU.mult)
            kb.op("dve", "tensor_reduce", out=sm[:, :, 0], in_=g3, axis=AX.X, op=ALU.add)
            kb.op("dve", "tensor_reduce", out=i3, in_=oh[:], axis=AX.X, op=ALU.add)
            kb.op("dve", "tensor_reduce", out=j3, in_=oh2[:], axis=AX.X, op=ALU.add)
            kb.op("dve", "reciprocal", out=sm[:, :, 1], in_=sm[:, :, 0])
            kb.tt("dve", g3, g3, bc(sm[:, :, 1:2], [128, 8, 16]), ALU.mult)
            for n_, (srcf, dstT) in enumerate(((i_f, iT), (j_f, jT), (g_f, gT))):
                kb.tr(ptr[:, n_ * 128:(n_ + 1) * 128], srcf[:], c.ident_f[:])
                kb.copy("act", dstT[:, r0:r0 + 128], ptr[:, n_ * 128:(n_ + 1) * 128])

        NTL = S // 128
        front(0)
        for tt in range(NTL):
            if tt + 1 < NTL:
                front(tt + 1)
            back(tt)
        kb.barrier()


def phase_peer_experts(c, l, rt, dst, dst_key, final_g=None):
    nc, kb, S = c.nc, c.kb, c.S
    iT, jT, gT = rt
    TB = 256
    NB = 3
    NPA = 3
    PD = 2
    NBLK = S // TB
    with ExitStack() as es:
        def SB(name, shape, dt):
            return es.enter_context(nc.sbuf_tensor(c.pfx + "e_" + name, shape, dt))

        def PS(name, shape, dt=F32):
            return es.enter_context(nc.psum_tensor(c.pfx + "e_" + name, shape, dt))
        G2 = [SB("G%d" % q, [128, TB, 128], BF16) for q in range(2)]
        xb = SB("xb", [128, 8, TB], BF16)
        As = [SB("A%d" % q, [128, 4, 128], BF16) for q in range(2)]
        Bs = [SB("B%d" % q, [128, 4, 128], BF16) for q in range(2)]
        gas = [SB("ga%d" % q, [128, TB], F32) for q in range(NPA)]
        was = [SB("wa%d" % q, [128, TB], BF16) for q in range(NPA)]
        ht = SB("h", [128, 1024], F32)
        houts = [SB("ho0", [128, 1024], F32), SB("ho1", [128, 1024], F32)]
        st = SB("st", [128, 4], F32)
        ut = [SB("ut%d" % q, [128, 2, 1024], BF16) for q in range(NB)]
        vt = [SB("vt%d" % q, [128, 2, 1024], BF16) for q in range(NB)]
        pG = [PS("pG%d" % q, [128, 512]) for q in range(2)]
        pA = [PS("pA%d" % q, [128, 512]) for q in range(2)]
        pO = [PS("pO%d" % q, [128, 1024]) for q in range(2)]
        if final_g is not None:
            gfin = SB("gf", [128, 1024], F32)
            kb.dma("sp", gfin[:], final_g.partition_broadcast(128))

        def g_dve(bk, u):
            A_, B_ = As[u % 2], Bs[u % 2]
            for t in range(4):
                tg = bk * TB + 4 * u + t
                kb.ts("dve", A_[:, t, :], c.iota_b[:], iT[:, tg:tg + 1], ALU.is_equal, gT[:, tg:tg + 1], ALU.mult)
                kb.ts("dve", B_[:, t, :], c.iota_b[:], jT[:, tg:tg + 1], ALU.is_equal)

        def g_pe(bk, u):
            A_, B_ = As[u % 2], Bs[u % 2]
            for t in range(4):
                kb.mm(pG[u % 2][:, t * 128:(t + 1) * 128], B_[:, t, :], A_[:, t, :], True, True)

        def g_evac(bk, u):
            kb.copy("act", G2[bk % 2][:, 4 * u:4 * u + 4, :], pG[u % 2][:].rearrange("p (u i) -> p u i", u=4))

        NU = TB // 4

        def g_slot(bk, s_):
            if bk >= NBLK:
                return
            if 0 <= s_ - 2 < NU:
                g_evac(bk, s_ - 2)
            if 0 <= s_ - 1 < NU:
                g_pe(bk, s_ - 1)
            if 0 <= s_ < NU:
                g_dve(bk, s_)

        for s_ in range(NU + 2):
            g_slot(0, s_)
        pending = None
        for b in range(NBLK):
            t0 = b * TB
            G = G2[b % 2]
            kb.dma("sp", xb[:], c.xnTd[b], rk=(("xnT", b, 0), ("xnT", b, 1)))

            def issue_act(ch):
                cp, u = ch // 2, ch % 2
                bu = ut[cp % NB]
                if u == 0:
                    kb.dma("sp", bu[:], c.utb[l, cp], rk=tuple(("utb", l, k) for k in range(32)))
                    kb.dma("sp", vt[cp % NB][:], c.vb[l, cp], rk=tuple(("vb", l, k) for k in range(32)))
                pa = pA[(ch // 2) % 2][:, (ch % 2) * TB:(ch % 2 + 1) * TB]
                for dk in range(8):
                    kb.mm(pa, bu[:, u, dk * 128:(dk + 1) * 128], xb[:, dk, :], dk == 0, dk == 7)
                kb.act(gas[ch % NPA][:], pa, AF.Gelu_apprx_tanh)
                kb.tt("dve", was[ch % NPA][:], gas[ch % NPA][:], G[:, :, ch], ALU.mult)

            def issue_v(ch):
                cp, u = ch // 2, ch % 2
                bv = vt[cp % NB]
                w_ = was[ch % NPA]
                for tt in range(2):
                    for half in range(2):
                        kb.mm(pO[tt][:, half * 512:(half + 1) * 512], w_[:, tt * 128:(tt + 1) * 128],
                              bv[:, u, half * 512:(half + 1) * 512], ch == 0, ch == 127)
            for ch in range(PD):
                issue_act(ch)
            if pending is not None:
                pending()
            for ch in range(128):
                if ch + PD < 128:
                    issue_act(ch + PD)
                issue_v(ch)
                if ch % 2 == 1:
                    g_slot(b + 1, ch // 2)
            g_slot(b + 1, NU)
            g_slot(b + 1, NU + 1)

            def epilogue(t0=t0):
                for tt in range(2):
                    r0 = t0 + tt * 128
                    kb.dma("sp", ht[:], c.hbuf[r0:r0 + 128, :], rk=(("h", r0 // 128),))
                    kb.tt("dve", houts[tt][:], pO[tt][:], ht[:], ALU.add)
                    if final_g is None:
                        kb.dma("sp", dst[r0:r0 + 128, :], houts[tt][:], wk=((dst_key, r0 // 128),))
                    else:
                        rmsnorm_tile(c, houts[tt][:], gfin[:], ht[:], st[:, 0:1], st[:, 1:2])
                        kb.dma("sp", dst[r0:r0 + 128, :], ht[:], wk=((dst_key, r0 // 128),))
            pending = epilogue
        pending()
        kb.barrier()


IN_SPECS = [
    ("x", None, F32),
    ("even_norm_g", [2, 1024], F32), ("even_w_in", [2, 1024, EVEN_IN], F32),
    ("lcw", [2, 128, 8, 4], F32), ("lcb", [2, 128, 8], F32),
    ("lru_gate_a_w", [2, 8, 128, 128], F32), ("gab", [2, 128, 8], F32),
    ("lru_gate_x_w", [2, 8, 128, 128], F32), ("gxb", [2, 128, 8], F32),
    ("lam", [2, 128, 8], F32),
    ("scw", [2, 128, 12, 4], F32), ("scb", [2, 128, 12], F32),
    ("ssd_dt_bias", [2, 16], F32), ("ssd_a_log", [2, 16], F32), ("ssd_d", [2, 16], F32),
    ("ssd_norm_g", [2, 1024], F32), ("even_w_out", [2, 2048, 1024], F32),
    ("odd_norm_g", [2, 1024], F32), ("odd_w_in", [2, 1024, 3072], F32), ("ocw", [2, 128, 8, 3], F32),
    ("odd_w_out", [2, 1024, 1024], F32),
    ("ffn_norm_g", [4, 1024], F32), ("peer_w_query", [4, 1024, 2048], F32),
    ("peer_sub_keys", [4, 8, 2, 128, 128], F32),
    ("peer_ut", [4, 128, 128, 1024], F32), ("peer_v", [4, 16384, 1024], F32),
    ("final_norm_g", [1, 1024], F32),
]


def build_program(S, plan, nlay=4):
    nc = bass.Bass("TRN2", target_bir_lowering=False)
    c = Ctx()
    c.nc, c.S = nc, S
    c.kb = KB(nc)
    c.d = {}
    for name, shape, dt in IN_SPECS:
        if name == "x":
            shape = [S, D]
        shape = list(shape)
        if name in ("peer_w_query", "peer_sub_keys", "peer_ut", "peer_v", "ffn_norm_g"):
            shape[0] = nlay
        c.d[name] = nc.dram_tensor(name, shape, dt, kind="ExternalInput").ap()
    c.y = nc.dram_tensor("y", [S, D], F32, kind="ExternalOutput").ap()
    c.hbuf = nc.dram_tensor("hbuf", [S, D], F32, kind="Internal").ap()
    c.xnTd = nc.dram_tensor("xnTd", [max(S // 256, 1), 128, 8, 256], BF16, kind="Internal").ap()
    c.utb = nc.dram_tensor("utb", [nlay, 64, 128, 2, 1024], BF16, kind="Internal").ap()
    c.vb = nc.dram_tensor("vb", [nlay, 64, 128, 2, 1024], BF16, kind="Internal").ap()
    c.yaTd = nc.dram_tensor("yaTd", [max(S // 256, 1), 128, 8, 256], BF16, kind="Internal").ap()
    setup_consts(c)
    c.kb.barrier()
    src, src_key = c.d["x"], "x"
    converted = set()

    def make_hook(n):
        def hook():
            for st_ in plan[n + 1:]:
                if st_[0] == "peer":
                    if st_[1] not in converted:
                        converted.add(st_[1])
                        convert_experts(c, st_[1])
                    break
        return hook
    for n, step in enumerate(plan):
        last = n == len(plan) - 1
        c.pfx = "p%d_" % n
        c.after_weights = make_hook(n)
        if step[0] == "peer" and step[1] not in converted:
            converted.add(step[1])
            convert_experts(c, step[1])
            c.kb.barrier()
        kind = step[0]
        if kind == "odd":
            phase_odd(c, step[1], src, src_key)
            src, src_key = c.hbuf, "h"
        elif kind == "even":
            phase_even_lru(c, step[1], src, src_key)
            phase_even_ssd(c, step[1], src, src_key)
            src, src_key = c.hbuf, "h"
        elif kind == "copy":
            phase_copy(c, src)
            src, src_key = c.hbuf, "h"
        elif kind == "peer":
            l = step[1]
            with ExitStack() as es:
                rt = tuple(es.enter_context(nc.sbuf_tensor(c.pfx + "rt_" + nm, [128, S], dt_))
                           for nm, dt_ in (("i", BF16), ("j", U8), ("g", BF16)))
                phase_peer_route(c, l, rt)
                if last:
                    fin = c.d["final_norm_g"][0, :] if (len(step) > 2 and step[2] == "final") else None
                    phase_peer_experts(c, l, rt, c.y, "y", final_g=fin)
                else:
                    phase_peer_experts(c, l, rt, c.hbuf, "h")
        if last and kind != "peer":
            phase_out(c)
    return nc


def convert_experts(c, l):
    kb = c.kb
    for pr in range(64):
        k = pr // 2
        kb.dma("pool", c.utb[l, pr].rearrange("p u n -> u p n"), c.d["peer_ut"][l, 2 * pr:2 * pr + 2], wk=(("utb", l, k),))
        kb.dma("pool", c.vb[l, pr].rearrange("p u n -> u p n"),
               c.d["peer_v"][l, 256 * pr:256 * (pr + 1), :].rearrange("(u p) n -> u p n", u=2), wk=(("vb", l, k),))


def phase_copy(c, src):
    nc, kb, S = c.nc, c.kb, c.S
    with nc.sbuf_tensor(c.pfx + "cp_t", [128, 1024], F32) as t:
        c.after_weights()
        for tt in range(S // 128):
            kb.dma("sp", t[:], src[tt * 128:(tt + 1) * 128, :])
            kb.dma("sp", c.hbuf[tt * 128:(tt + 1) * 128, :], t[:], wk=(("h", tt),))
        kb.barrier()


def phase_out(c):
    nc, kb, S = c.nc, c.kb, c.S
    with nc.sbuf_tensor(c.pfx + "out_t", [128, 1024], F32) as t:
        for tt in range(S // 128):
            kb.dma("sp", t[:], c.hbuf[tt * 128:(tt + 1) * 128, :], rk=(("h", tt),))
            kb.dma("sp", c.y[tt * 128:(tt + 1) * 128, :], t[:], wk=(("y", tt),))
        kb.barrier()


def phase_even(c, i, src, src_key):
    raise NotImplementedError


def phase_even_lru(c, i, src, src_key):
    nc, kb, S = c.nc, c.kb, c.S
    TS = 512
    NT = TS // 128
    with ExitStack() as es:
        def SB(name, shape, dt):
            return es.enter_context(nc.sbuf_tensor(c.pfx + "l_" + name, shape, dt))

        def PS(name, shape, dt=F32):
            return es.enter_context(nc.psum_tensor(c.pfx + "l_" + name, shape, dt))
        wi = SB("wi", [128, 8, 2048], BF16)
        gaw = SB("gaw", [128, 8, 128], BF16)
        gxw = SB("gxw", [128, 8, 128], BF16)
        lcw = SB("lcw", [128, 8, 4], F32)
        lcb = SB("lcb", [128, 8], F32)
        gab = SB("gab", [128, 8], F32)
        gxb = SB("gxb", [128, 8], F32)
        lam = SB("lam", [128, 8], F32)
        c1 = SB("c1", [128, 8], F32)
        c2 = SB("c2", [128, 8], F32)
        hl = SB("hl", [128, 8], F32)
        gbc = SB("g", [128, 1024], F32)
        ht2 = [SB("h%d" % q, [128, 1024], F32) for q in range(2)]
        xn = SB("xn", [128, 1024], BF16)
        xnT2 = [SB("xnT%d" % q, [128, 8, TS], BF16) for q in range(2)]
        yaT = SB("yaT", [128, 8, TS], BF16)
        st = SB("st", [128, 4], F32)
        xlb = [SB("xlb%d" % f, [128, TS + 3], F32) for f in range(8)]
        gl8, xa8, r8, ig8, a8, a28, bb8, hsq8 = [SB(n_, [128, 4, TS], F32) for n_ in ("gl", "xa", "r", "ig", "a", "a2", "bb", "hsq")]
        xab8 = SB("xab", [128, 4, TS], BF16)
        psT = PS("psT", [128, 1024], BF16)
        pgs = [PS("pg%d" % q, [128, 512]) for q in range(2)]
        pxs = [PS("px%d" % q, [128, 512]) for q in range(2)]

        for dk in range(8):
            kb.dma("pool", wi[:, dk, :], c.d["even_w_in"][i, dk * 128:(dk + 1) * 128, 0:2048])
        kb.dma("pool", gaw[:], c.d["lru_gate_a_w"][i].rearrange("h i j -> i h j"))
        kb.dma("pool", gxw[:], c.d["lru_gate_x_w"][i].rearrange("h i j -> i h j"))
        kb.dma("sp", lcw[:], c.d["lcw"][i])
        kb.dma("sp", lcb[:], c.d["lcb"][i])
        kb.dma("sp", gab[:], c.d["gab"][i])
        kb.dma("sp", gxb[:], c.d["gxb"][i])
        kb.dma("sp", lam[:], c.d["lam"][i])
        kb.dma("sp", gbc[:], c.d["even_norm_g"][i, :].partition_broadcast(128))
        c.after_weights()
        kb.act(c1[:], lam[:], AF.Exp, scale=-1.0)
        kb.act(c1[:], c1[:], AF.Ln, bias=1.0)
        kb.ts("dve", c2[:], c1[:], -16.0, ALU.mult)
        kb.ts("dve", c1[:], c1[:], -8.0, ALU.mult)
        kb.op("dve", "memset", ap=hl[:], constant=0.0, wk=(hl.name,))
        for f in range(8):
            kb.op("dve", "memset", ap=xlb[f][:, 0:3], constant=0.0, wk=(xlb[f].name,))
        NSC = S // TS

        def front(sc):
            for tt in range(NT):
                r0 = sc * TS + tt * 128
                ht = ht2[tt % 2]
                kb.dma("sp", ht[:], src[r0:r0 + 128, :], rk=((src_key, r0 // 128),))
                rmsnorm_tile(c, ht[:], gbc[:], xn[:], st[:, 0:1], st[:, 1:2])
                transpose_to(c, xn, xnT2[sc % 2][:, :, tt * 128:(tt + 1) * 128], psT)

        front(0)
        for sc in range(NSC):
            xnT = xnT2[sc % 2]
            for g in range(2):
                for fl in range(4):
                    f = 4 * g + fl
                    xb_ = xlb[f]
                    pg_, px_ = pgs[f % 2], pxs[f % 2]
                    for dk in range(8):
                        kb.mm(pg_[:, 0:TS], wi[:, dk, f * 128:(f + 1) * 128], xnT[:, dk, :], dk == 0, dk == 7)
                    for dk in range(8):
                        kb.mm(px_[:, 0:TS], wi[:, dk, 1024 + f * 128:1024 + (f + 1) * 128], xnT[:, dk, :], dk == 0, dk == 7)
                    kb.act(gl8[:, fl, :], pg_[:, 0:TS], AF.Gelu_apprx_tanh)
                    kb.copy("act", xb_[:, 3:TS + 3], px_[:, 0:TS])
                    kb.ts("dve", xa8[:, fl, :], xb_[:, 3:TS + 3], lcw[:, f, 3:4], ALU.mult, lcb[:, f:f + 1], ALU.add)
                    for k in range(3):
                        kb.stt(xa8[:, fl, :], xb_[:, k:k + TS], lcw[:, f, k:k + 1], xa8[:, fl, :], ALU.mult, ALU.add)
                    kb.copy("act", xb_[:, 0:3], xb_[:, TS:TS + 3])
                    kb.copy("act", xab8[:, fl, :], xa8[:, fl, :])
                if g == 1 and sc + 1 < NSC:
                    front(sc + 1)
                for fl in range(4):
                    f = 4 * g + fl
                    pg_, px_ = pgs[f % 2], pxs[f % 2]
                    kb.mm(pg_[:, 0:TS], gaw[:, f, :], xab8[:, fl, :], True, True)
                    kb.mm(px_[:, 0:TS], gxw[:, f, :], xab8[:, fl, :], True, True)
                    kb.act(r8[:, fl, :], pg_[:, 0:TS], AF.Sigmoid, bias=gab[:, f:f + 1])
                    kb.act(ig8[:, fl, :], px_[:, 0:TS], AF.Sigmoid, bias=gxb[:, f:f + 1])
                for fl in range(4):
                    f = 4 * g + fl
                    kb.act(a8[:, fl, :], r8[:, fl, :], AF.Exp, scale=c1[:, f:f + 1])
                    kb.act(a28[:, fl, :], r8[:, fl, :], AF.Exp, scale=c2[:, f:f + 1])
                kb.ts("dve", a28[:], a28[:], 1.0, ALU.min, -1.0, ALU.mult)
                kb.act(a28[:], a28[:], AF.Sqrt, bias=1.0)
                kb.tt("dve", bb8[:], ig8[:], xa8[:], ALU.mult)
                kb.tt("dve", bb8[:], bb8[:], a28[:], ALU.mult)
                for fl in range(4):
                    f = 4 * g + fl
                    kb.op("dve", "tensor_tensor_scan", out=hsq8[:, fl, :], data0=a8[:, fl, :], data1=bb8[:, fl, :],
                          initial=hl[:, f:f + 1], op0=ALU.mult, op1=ALU.add)
                kb.copy("act", hl[:, 4 * g:4 * g + 4], hsq8[:, :, TS - 1])
                kb.tt("dve", yaT[:, 4 * g:4 * g + 4, :], gl8[:], hsq8[:], ALU.mult)
            for hb in range(TS // 256):
                blk = sc * (TS // 256) + hb
                kb.dma("sp", c.yaTd[blk], yaT[:, :, hb * 256:(hb + 1) * 256], wk=(("yaT", blk),))
        kb.barrier()


def phase_even_ssd(c, i, src, src_key):
    nc, kb, S = c.nc, c.kb, c.S
    TS = 256
    NT = TS // 128
    XO = 3072
    with ExitStack() as es:
        def SB(name, shape, dt):
            return es.enter_context(nc.sbuf_tensor(c.pfx + "s_" + name, shape, dt))

        def PS(name, shape, dt=F32):
            return es.enter_context(nc.psum_tensor(c.pfx + "s_" + name, shape, dt))
        wi = SB("wi", [128, 8, 2576], BF16)
        wo = SB("wo", [128, 16, 1024], BF16)
        scw = SB("scw", [128, 12, 4], F32)
        scb = SB("scb", [128, 12], F32)
        gbc = SB("g", [128, 1024], F32)
        ngbc = SB("ng", [128, 1024], F32)
        dtb = SB("dtb", [128, 16], F32)
        aneg = SB("aneg", [128, 16], F32)
        dsk = SB("dsk", [128, 16], F32)
        hres2 = [SB("hres%d" % q, [128, NT, 1024], F32) for q in range(2)]
        xn = SB("xn", [128, 1024], BF16)
        xnT2 = [SB("xnT%d" % q, [128, 8, TS], BF16) for q in range(2)]
        yabT2 = [SB("yabT%d" % q, [128, 16, TS], BF16) for q in range(2)]
        xsb = [SB("xsb%d" % f, [128, TS + 3], F32) for f in range(12)]
        cs2 = [SB("cs%d" % q, [128, TS], F32) for q in range(2)]
        xsT = SB("xsT", [128, 8, TS], F32)
        BT = SB("BT", [128, 2, TS], BF16)
        CT = SB("CT", [128, 2, TS], BF16)
        sz2 = [SB("sz%d" % q, [128, 1024], F32) for q in range(2)]
        xs_tok = SB("xs_tok", [128, 1024], F32)
        Btok = SB("Btok", [128, 256], BF16)
        xdt = SB("xdt", [128, 16, 64], BF16)
        xdtw = SB("xdtw", [128, 16, 64], BF16)
        mcb = SB("mcb", [128, 2, 128], F32)
        lseg4 = [SB("lseg%d" % q, [128, 4, 128], F32) for q in range(2)]
        E4 = [SB("E%d" % q, [128, 512], F32) for q in range(2)]
        MT4 = [SB("MT%d" % q, [128, 512], BF16) for q in range(2)]
        ysb = SB("ysb", [128, 1024], F32)
        tmp = SB("tmp", [128, 1024], F32)
        hs_f = SB("hs_f", [128, 1024], F32)
        hs_b = SB("hs_b", [128, 1024], BF16)
        yb = SB("yb", [128, 1024], BF16)
        hout2 = [SB("hout%d" % q, [128, 1024], F32) for q in range(2)]
        sm = SB("sm", [128, 8, 16], F32)
        ex = SB("ex", [128, 48], F32)
        st = SB("st", [128, 4], F32)
        psT = PS("psT", [128, 1024], BF16)
        pfa = PS("pfa", [128, 512])
        pfb = PS("pfb", [128, 512])
        pb1 = PS("pb1", [128, 1024])
        pb2 = PS("pb2", [128, 1024])
        psm = PS("psm", [128, 512])

        for dk in range(8):
            kb.dma("pool", wi[:, dk, :], c.d["even_w_in"][i, dk * 128:(dk + 1) * 128, 2048:4624])
        load_w_bf16(c, wo, c.d["even_w_out"][i], 16)
        kb.dma("sp", scw[:], c.d["scw"][i])
        kb.dma("sp", scb[:], c.d["scb"][i])
        kb.dma("sp", gbc[:], c.d["even_norm_g"][i, :].partition_broadcast(128))
        kb.dma("sp", ngbc[:], c.d["ssd_norm_g"][i, :].partition_broadcast(128))
        kb.dma("sp", dtb[:], c.d["ssd_dt_bias"][i, :].partition_broadcast(128))
        kb.dma("sp", aneg[:], c.d["ssd_a_log"][i, :].partition_broadcast(128))
        kb.dma("sp", dsk[:], c.d["ssd_d"][i, :].partition_broadcast(128))
        kb.act(aneg[:], aneg[:], AF.Exp)
        kb.ts("dve", aneg[:], aneg[:], -1.0, ALU.mult)
        kb.op("dve", "memset", ap=hs_f[:], constant=0.0, wk=(hs_f.name,))
        kb.op("dve", "memset", ap=hs_b[:], constant=0.0, wk=(hs_b.name,))
        for f in range(12):
            kb.op("dve", "memset", ap=xsb[f][:, 0:3], constant=0.0, wk=(xsb[f].name,))
        ZO, XB, DT0 = 0, 1024, 2560
        NSC = S // TS

        def front(sc):
            q = sc % 2
            kb.dma("sp", yabT2[q][:, 0:8, :], c.yaTd[sc], rk=(("yaT", sc),))
            for tt in range(NT):
                r0 = sc * TS + tt * 128
                kb.dma("sp", hres2[q][:, tt, :], src[r0:r0 + 128, :], rk=((src_key, r0 // 128),))
                rmsnorm_tile(c, hres2[q][:, tt, :], gbc[:], xn[:], st[:, 0:1], st[:, 1:2])
                transpose_to(c, xn, xnT2[q][:, :, tt * 128:(tt + 1) * 128], psT)

        def back(sc):
            q = sc % 2
            for tt in range(NT):
                r0 = sc * TS + tt * 128
                for half in range(2):
                    for kc in range(16):
                        kb.mm(pb2[:, half * 512:(half + 1) * 512], yabT2[q][:, kc, tt * 128:(tt + 1) * 128],
                              wo[:, kc, half * 512:(half + 1) * 512], kc == 0, kc == 15)
                kb.tt("dve", hout2[tt % 2][:], pb2[:], hres2[q][:, tt, :], ALU.add)
                kb.dma("sp", c.hbuf[r0:r0 + 128, :], hout2[tt % 2][:], wk=(("h", r0 // 128),))

        front(0)
        for sc in range(NSC):
            t0 = sc * TS
            hres, xnT, yabT = hres2[sc % 2], xnT2[sc % 2], yabT2[sc % 2]
            for f in range(12):
                pf = pfa if f % 2 == 0 else pfb
                xb_ = xsb[f]
                for dk in range(8):
                    kb.mm(pf[:, 0:TS], wi[:, dk, XB + f * 128:XB + (f + 1) * 128], xnT[:, dk, :], dk == 0, dk == 7)
                cs = cs2[f % 2]
                kb.copy("act", xb_[:, 3:TS + 3], pf[:, 0:TS])
                kb.ts("dve", cs[:], xb_[:, 3:TS + 3], scw[:, f, 3:4], ALU.mult, scb[:, f:f + 1], ALU.add)
                for k in range(3):
                    kb.stt(cs[:], xb_[:, k:k + TS], scw[:, f, k:k + 1], cs[:], ALU.mult, ALU.add)
                kb.copy("act", xb_[:, 0:3], xb_[:, TS:TS + 3])
                if f < 8:
                    dst = xsT[:, f, :]
                elif f < 10:
                    dst = BT[:, f - 8, :]
                else:
                    dst = CT[:, f - 10, :]
                kb.act(dst, cs[:], AF.Silu)
                if f == 3 and sc > 0:
                    back(sc - 1)
                if f == 8 and sc + 1 < NSC:
                    front(sc + 1)
            for tt in range(NT):
                cols = slice(tt * 128, (tt + 1) * 128)
                for half in range(2):
                    for dk in range(8):
                        kb.mm(pb1[:, half * 512:(half + 1) * 512], xnT[:, dk, cols],
                              wi[:, dk, ZO + half * 512:ZO + (half + 1) * 512], dk == 0, dk == 7)
                kb.act(sz2[tt][:], pb1[:], AF.Silu)
            for tt in range(NT):
                cols = slice(tt * 128, (tt + 1) * 128)
                sz = sz2[tt]
                pdt = psm[:, 432:448]
                for dk in range(8):
                    kb.mm(pdt, xnT[:, dk, cols], wi[:, dk, DT0:DT0 + 16], dk == 0, dk == 7)
                t1, ax, dt_, dA, dtw = sm[:, 0, :], sm[:, 1, :], sm[:, 2, :], sm[:, 3, :], sm[:, 4, :]
                kb.tt("dve", t1, pdt, dtb[:], ALU.add)
                kb.ts("dve", ax, t1, -1.0, ALU.mult)
                kb.tt("dve", ax, ax, t1, ALU.max)
                kb.act(ax, ax, AF.Exp, scale=-1.0)
                kb.act(ax, ax, AF.Ln, bias=1.0)
                kb.stt(dt_, t1, 0.0, ax, ALU.max, ALU.add)
                kb.tt("dve", dA, dt_, aneg[:], ALU.mult)
                for f in range(8):
                    kb.tr(pb2[:, f * 128:(f + 1) * 128], xsT[:, f, cols], c.ident_f[:])
                for g in range(2):
                    kb.tr(psT[:, g * 128:(g + 1) * 128], BT[:, g, cols], c.ident_b[:])
                for g in range(2):
                    kb.mm(psm[:, g * 128:(g + 1) * 128], BT[:, g, cols], CT[:, g, cols], True, True)
                kb.mm(psm[:, 384:400], c.triU[:], dA, True, True)
                kb.mm(psm[:, 400:416], c.maskgt[:], dA, True, True)
                kb.mm(psm[:, 416:432], c.ones_f[:], dA, True, True)
                kb.copy("act", xs_tok[:], pb2[:])
                kb.copy("act", Btok[:], psT[:, 0:256])
                kb.act(ex[:], psm[:, 384:432], AF.Exp)
                expcum, dte, cd = ex[:, 0:16], ex[:, 16:32], ex[:, 32:48]
                kb.tt("dve", dtw, dt_, dte, ALU.mult)
                kb.tt("dve", mcb[:], psm[:, 0:256].rearrange("p (g l) -> p g l", g=2),
                      bc(c.triU[:].unsqueeze(1), [128, 2, 128]), ALU.mult)
                xs3 = xs_tok[:].rearrange("p (j q) -> p j q", j=16)
                kb.tt("dve", xdt[:], xs3, bc(dt_.unsqueeze(2), [128, 16, 64]), ALU.mult)
                kb.tt("dve", xdtw[:], xs3, bc(dtw.unsqueeze(2), [128, 16, 64]), ALU.mult)
                def hg_dve(q):
                    j0 = q * 4
                    kb.tt("dve", lseg4[q % 2][:], bc(c.maskgt[:].unsqueeze(1), [128, 4, 128]),
                          bc(dA[:, j0:j0 + 4].unsqueeze(2), [128, 4, 128]), ALU.mult)

                def hg_seg(q):
                    pf = (pfa, pfb)[q % 2]
                    for u in range(4):
                        kb.mm(pf[:, u * 128:(u + 1) * 128], lseg4[q % 2][:, u, :], c.triU[:], True, True)

                def hg_exp(q):
                    kb.act(E4[q % 2][:], (pfa, pfb)[q % 2][:], AF.Exp)

                def hg_mt(q):
                    g = (q * 4) // 8
                    kb.tt("pool", MT4[q % 2][:].rearrange("p (u l) -> p u l", u=4),
                          E4[q % 2][:].rearrange("p (u l) -> p u l", u=4),
                          bc(mcb[:, g, :].unsqueeze(1), [128, 4, 128]), ALU.mult)

                def hg_y(q):
                    for u in range(4):
                        j = q * 4 + u
                        kb.mm(pb1[:, j * 64:(j + 1) * 64], MT4[q % 2][:, u * 128:(u + 1) * 128], xdt[:, j, :], True, True)
                for pr_ in range(2):
                    qa, qb = 2 * pr_, 2 * pr_ + 1
                    for fn in (hg_dve, hg_seg, hg_exp, hg_mt, hg_y):
                        fn(qa)
                        fn(qb)
                for g in range(2):
                    kb.mm(pb2[:, g * 512:(g + 1) * 512], CT[:, g, cols], hs_b[:, g * 512:(g + 1) * 512], True, True)
                for j in range(16):
                    kb.op("act", "activation", out=xs_tok[:, j * 64:(j + 1) * 64], in_=xs_tok[:, j * 64:(j + 1) * 64],
                          func=AF.Copy, scale=dsk[:, j:j + 1],
                          xr=({xs_tok.name, dsk.name} if j == 0 else set()), xw={xs_tok.name})
                hs3 = hs_f[:].rearrange("p (j q) -> p j q", j=16)
                kb.tt("pool", hs3, hs3, bc(cd.unsqueeze(2), [128, 16, 64]), ALU.mult)
                kb.tt("dve", tmp[:].rearrange("p (j q) -> p j q", j=16), pb2[:].rearrange("p (j q) -> p j q", j=16),
                      bc(expcum.unsqueeze(2), [128, 16, 64]), ALU.mult)
                kb.tt("dve", ysb[:], pb1[:], tmp[:], ALU.add)
                kb.tt("dve", ysb[:], ysb[:], xs_tok[:], ALU.add)
                for g in range(2):
                    kb.mm(pb2[:, g * 512:(g + 1) * 512], Btok[:, g * 128:(g + 1) * 128],
                          xdtw[:, g * 8:(g + 1) * 8, :].rearrange("p j q -> p (j q)"), True, True)
                kb.tt("dve", hs_f[:], hs_f[:], pb2[:], ALU.add)
                kb.copy("act", hs_b[:], hs_f[:])
                kb.tt("dve", tmp[:], ysb[:], sz[:], ALU.mult)
                rmsnorm_tile(c, tmp[:], ngbc[:], yb[:], st[:, 2:3], st[:, 3:4])
                for k in range(8):
                    kb.tr(psT[:, k * 128:(k + 1) * 128], yb[:, k * 128:(k + 1) * 128], c.ident_b[:])
                kb.copy("act", yabT[:, 8:16, cols], psT[:].rearrange("p (k t) -> p k t", k=8))
        back(NSC - 1)
        kb.barrier()


def _fm(a, k_last=True):
    a = np.asarray(a)
    if a.ndim == 2:
        E, C = a.shape
        return np.ascontiguousarray(a.reshape(E, C // 128, 128).transpose(0, 2, 1))
    E, K, C = a.shape
    return np.ascontiguousarray(a.reshape(E, K, C // 128, 128).transpose(0, 3, 2, 1))


def prep_shared(inp, layers=(0, 1, 2, 3)):
    layers = list(layers)
    sh = {}
    for k in ("even_norm_g", "even_w_in", "lru_gate_a_w", "lru_gate_x_w", "ssd_dt_bias", "ssd_a_log", "ssd_d",
              "ssd_norm_g", "even_w_out", "odd_norm_g", "odd_w_in", "odd_w_out"):
        sh[k] = np.ascontiguousarray(inp[k])
    sh["lcw"] = _fm(inp["lru_conv_w"])
    sh["lcb"] = _fm(inp["lru_conv_b"])
    sh["gab"] = _fm(inp["lru_gate_a_b"])
    sh["gxb"] = _fm(inp["lru_gate_x_b"])
    sh["lam"] = _fm(inp["lru_lambda"])
    sh["scw"] = _fm(inp["ssd_conv_w"])
    sh["scb"] = _fm(inp["ssd_conv_b"])
    sh["ocw"] = _fm(inp["odd_conv_w"])
    sh["ffn_norm_g"] = np.ascontiguousarray(np.asarray(inp["ffn_norm_g"])[layers])
    sh["peer_w_query"] = np.ascontiguousarray(np.asarray(inp["peer_w_query"])[layers])
    sh["peer_sub_keys"] = np.ascontiguousarray(np.asarray(inp["peer_sub_keys"])[layers])
    u = np.asarray(inp["peer_u"])[layers]
    L = u.shape[0]
    sh["peer_ut"] = np.ascontiguousarray(u.reshape(L, 128, 128, 8, 128).transpose(0, 1, 4, 3, 2)).reshape(L, 128, 128, 1024)
    sh["peer_v"] = np.ascontiguousarray(np.asarray(inp["peer_v"])[layers])
    sh["final_norm_g"] = np.ascontiguousarray(np.asarray(inp["final_norm_g"]).reshape(1, 1024))
    return sh


FULL_PLAN = [("even", 0), ("peer", 0), ("odd", 0), ("peer", 1), ("even", 1), ("peer", 2), ("odd", 1), ("peer", 3, "final")]


def kernel(**inputs):
    x = np.asarray(inputs["x"])
    B, S, _ = x.shape
    sh = prep_shared(inputs)
    nc = build_program(S, FULL_PLAN)
    in_maps = []
    for b in range(B):
        m = dict(sh)
        m["x"] = np.ascontiguousarray(x[b])
        in_maps.append(m)
    res = run_bass_kernel_spmd(nc, in_maps, core_ids=list(range(B)))
    return np.stack([np.asarray(r["y"]) for r in res.results], axis=0).astype(np.float32)
```
